# Optimizing a Trainium2 kernel written in Bass

```python
import math
import jax, jax.numpy as jnp
from jax import lax
import numpy as np

D_MODEL = 1024
BATCH = 16
SEQ = 2048
DEPTH = 4

BLOCK = 128
RET_HEADS = 4
RET_DIM = 128
SB_HEADS = 8
SB_DIM = 64
DIFF_HEADS = 8
DIFF_DIM = 64
REL_BUCKETS = 32
REL_MAX_DIST = 128
D_FF = 2816
CONV_W = 3
ALPHA = (2 * DEPTH) ** 0.25
BETA = (8 * DEPTH) ** -0.25
LN_EPS = 1e-5
ROPE_BASE = 10000.0

RET_W = RET_HEADS * RET_DIM
SB_W = SB_HEADS * SB_DIM
EVEN_IN = 4 * RET_W + 3 * SB_W
EVEN_SPLITS = [RET_W, 2 * RET_W, 3 * RET_W, 4 * RET_W, 4 * RET_W + SB_W, 4 * RET_W + 2 * SB_W]
DIFF_W = DIFF_HEADS * 2 * DIFF_DIM
ODD_IN = 3 * DIFF_W
N_EVEN = (DEPTH + 1) // 2
N_ODD = DEPTH // 2

kernel_name = "retention_stickbreak_diffattn_convffn_deepnorm"


def layer_norm(x, g, b):
    xf = x.astype(jnp.float32)
    mu = xf.mean(-1, keepdims=True)
    var = jnp.square(xf - mu).mean(-1, keepdims=True)
    return ((xf - mu) * lax.rsqrt(var + LN_EPS) * g.astype(jnp.float32) + b.astype(jnp.float32)).astype(x.dtype)


def rotary(x):
    S, dh = x.shape[1], x.shape[-1]
    inv = ROPE_BASE ** (-jnp.arange(0, dh, 2, dtype=jnp.float32) / dh)
    ang = jnp.arange(S, dtype=jnp.float32)[:, None] * inv[None, :]
    cos = jnp.cos(ang)[None, :, None, :]
    sin = jnp.sin(ang)[None, :, None, :]
    xf = x.astype(jnp.float32)
    x1, x2 = xf[..., : dh // 2], xf[..., dh // 2:]
    return jnp.concatenate([x1 * cos - x2 * sin, x1 * sin + x2 * cos], -1).astype(x.dtype)


def retention(q, k, v):
    bsz, S, H, dh = q.shape
    n = S // BLOCK
    log_g = jnp.log(1.0 - 2.0 ** (-5.0 - jnp.arange(H, dtype=jnp.float32)))
    idx = jnp.arange(BLOCK, dtype=jnp.float32)
    rel = idx[:, None] - idx[None, :]
    decay = jnp.where(rel >= 0, jnp.exp(log_g[:, None, None] * jnp.maximum(rel, 0.0)), 0.0)
    q_dec = jnp.exp(log_g[:, None] * (idx[None, :] + 1.0))
    k_dec = jnp.exp(log_g[:, None] * (BLOCK - 1.0 - idx[None, :]))
    chunk_g = jnp.exp(log_g * BLOCK)

    def chunks(t):
        return t.astype(jnp.float32).reshape(bsz, n, BLOCK, H, dh).transpose(1, 0, 3, 2, 4)

    qc, kc, vc = chunks(q), chunks(k) * (dh ** -0.5), chunks(v)

    def step(state, inp):
        qi, ki, vi = inp
        s = jnp.einsum('bhid,bhjd->bhij', qi, ki) * decay[None]
        intra = jnp.einsum('bhij,bhjd->bhid', s, vi)
        cross = jnp.einsum('bhid,bhde->bhie', qi, state) * q_dec[None, :, :, None]
        new_state = state * chunk_g[None, :, None, None] + jnp.einsum(
            'bhjd,bhje->bhde', ki * k_dec[None, :, :, None], vi)
        return new_state, intra + cross

    state0 = jnp.zeros((bsz, H, dh, dh), jnp.float32)
    _, out = lax.scan(step, state0, (qc, kc, vc))
    return out.transpose(1, 0, 3, 2, 4).reshape(bsz, S, H, dh)


def stick_breaking(q, k, v):
    bsz, S, H, dh = q.shape
    n = S // BLOCK
    qb = q.reshape(bsz, n, BLOCK, H, dh).transpose(1, 0, 3, 2, 4)
    kh = k.transpose(0, 2, 1, 3)
    vh = v.transpose(0, 2, 1, 3)
    kpos = jnp.arange(S)
    scale = dh ** -0.5

    def block(args):
        qi, start = args
        qpos = start + jnp.arange(BLOCK)
        z = jnp.einsum('bhid,bhsd->bhis', qi, kh).astype(jnp.float32) * scale
        past = (kpos[None, :] < qpos[:, None])[None, None]
        log_keep = jnp.where(past, jax.nn.log_sigmoid(-z), 0.0)
        between = lax.cumsum(log_keep, axis=3, reverse=True) - log_keep
        a = jnp.where(past, jnp.exp(jax.nn.log_sigmoid(z) + between), 0.0)
        return jnp.einsum('bhis,bhsd->bhid', a.astype(vh.dtype), vh)

    out = lax.map(block, (qb, jnp.arange(n) * BLOCK))
    return out.transpose(1, 0, 3, 2, 4).reshape(bsz, S, H * dh)


def t5_bucket(rel):
    n = jnp.maximum(rel, 0)
    max_exact = REL_BUCKETS // 2
    nf = jnp.maximum(n, 1).astype(jnp.float32)
    large = max_exact + (jnp.log(nf / max_exact) / math.log(REL_MAX_DIST / max_exact)
                         * (REL_BUCKETS - max_exact)).astype(jnp.int32)
    large = jnp.minimum(large, REL_BUCKETS - 1)
    return jnp.where(n < max_exact, n, large)


def diff_attention(q, k, v, lam, rel_bias):
    _, bsz, H, S, dh = q.shape
    n = S // BLOCK
    qb = q.reshape(2, bsz, H, n, BLOCK, dh).transpose(3, 0, 1, 2, 4, 5)
    kpos = jnp.arange(S)
    scale = dh ** -0.5
    table = rel_bias.astype(jnp.float32)

    def block(args):
        qi, start = args
        qpos = start + jnp.arange(BLOCK)
        rel = qpos[:, None] - kpos[None, :]
        bias = table[t5_bucket(rel)].transpose(2, 0, 1)
        s = jnp.einsum('pbhid,pbhsd->pbhis', qi, k).astype(jnp.float32) * scale + bias[None, None]
        s = jnp.where((rel >= 0)[None, None, None], s, -jnp.inf)
        p = jax.nn.softmax(s, axis=-1)
        w = p[0] - lam * p[1]
        return jnp.einsum('bhis,bhse->bhie', w.astype(v.dtype), v).astype(jnp.float32)

    out = lax.map(block, (qb, jnp.arange(n) * BLOCK))
    return out.transpose(1, 2, 0, 3, 4).reshape(bsz, H, S, 2 * dh)


def even_mixer(x, w_in, w_out):
    bsz, S, _ = x.shape
    h = x @ w_in
    rq, rk, rv, rg, sq, sk, sv = jnp.split(h, EVEN_SPLITS, axis=-1)
    hd = lambda t, nh, dh: t.reshape(bsz, S, nh, dh)
    ret = retention(rotary(hd(rq, RET_HEADS, RET_DIM)), rotary(hd(rk, RET_HEADS, RET_DIM)),
                    hd(rv, RET_HEADS, RET_DIM))
    mu = ret.mean(-1, keepdims=True)
    var = jnp.square(ret - mu).mean(-1, keepdims=True)
    ret = ((ret - mu) * lax.rsqrt(var + LN_EPS)).reshape(bsz, S, RET_W)
    ret = jax.nn.silu(rg.astype(jnp.float32)) * ret
    sb = stick_breaking(hd(sq, SB_HEADS, SB_DIM), hd(sk, SB_HEADS, SB_DIM), hd(sv, SB_HEADS, SB_DIM))
    y = jnp.concatenate([ret.astype(x.dtype), sb.astype(x.dtype)], axis=-1)
    return y @ w_out


def odd_mixer(x, w_in, w_out, lq1, lk1, lq2, lk2, sub_g, rel_bias, layer):
    bsz, S, _ = x.shape
    h = x @ w_in
    q, k, v = jnp.split(h, 3, axis=-1)
    q = q.reshape(bsz, S, DIFF_HEADS, 2, DIFF_DIM).transpose(3, 0, 2, 1, 4)
    k = k.reshape(bsz, S, DIFF_HEADS, 2, DIFF_DIM).transpose(3, 0, 2, 1, 4)
    v = v.reshape(bsz, S, DIFF_HEADS, 2 * DIFF_DIM).transpose(0, 2, 1, 3)
    lam_init = 0.8 - 0.6 * math.exp(-0.3 * layer)
    lam = (jnp.exp(jnp.sum(lq1.astype(jnp.float32) * lk1.astype(jnp.float32)))
           - jnp.exp(jnp.sum(lq2.astype(jnp.float32) * lk2.astype(jnp.float32))) + lam_init)
    o = diff_attention(q, k, v, lam, rel_bias)
    o = o * lax.rsqrt(jnp.square(o).mean(-1, keepdims=True) + LN_EPS) * sub_g.astype(jnp.float32)
    o = (o * (1.0 - lam_init)).transpose(0, 2, 1, 3).reshape(bsz, S, DIFF_W).astype(x.dtype)
    return o @ w_out


def conv_ffn(x, w_up, conv_w, conv_b, w_down):
    S = x.shape[1]
    h = x @ w_up
    hp = jnp.pad(h, ((0, 0), (CONV_W - 1, 0), (0, 0)))
    h = sum(hp[:, j:j + S] * conv_w[j] for j in range(CONV_W)) + conv_b
    u, g = jnp.split(h, 2, axis=-1)
    return (jax.nn.silu(g) * u) @ w_down


def setup_inputs(seed: int = 0) -> dict:
    key = jax.random.key(seed)
    ks = jax.random.split(key, 20)
    nrm = lambda k, shape, s: jax.random.normal(k, shape, jnp.float32) * s
    return {
        "x": nrm(ks[0], (BATCH, SEQ, D_MODEL), 1.0),
        "w_in_even": nrm(ks[1], (N_EVEN, D_MODEL, EVEN_IN), D_MODEL ** -0.5),
        "w_out_even": nrm(ks[2], (N_EVEN, RET_W + SB_W, D_MODEL), (RET_W + SB_W) ** -0.5 * BETA),
        "w_in_odd": nrm(ks[3], (N_ODD, D_MODEL, ODD_IN), D_MODEL ** -0.5),
        "w_out_odd": nrm(ks[4], (N_ODD, DIFF_W, D_MODEL), DIFF_W ** -0.5 * BETA),
        "lam_q1": nrm(ks[5], (N_ODD, DIFF_DIM), 0.1),
        "lam_k1": nrm(ks[6], (N_ODD, DIFF_DIM), 0.1),
        "lam_q2": nrm(ks[7], (N_ODD, DIFF_DIM), 0.1),
        "lam_k2": nrm(ks[8], (N_ODD, DIFF_DIM), 0.1),
        "subln_g": 1.0 + nrm(ks[9], (N_ODD, 2 * DIFF_DIM), 0.02),
        "rel_bias": nrm(ks[10], (REL_BUCKETS, DIFF_HEADS), 0.5),
        "w_up": nrm(ks[11], (DEPTH, D_MODEL, 2 * D_FF), D_MODEL ** -0.5),
        "conv_w": nrm(ks[12], (DEPTH, CONV_W, 2 * D_FF), CONV_W ** -0.5),
        "conv_b": nrm(ks[13], (DEPTH, 2 * D_FF), 0.02),
        "w_down": nrm(ks[14], (DEPTH, D_FF, D_MODEL), D_FF ** -0.5 * BETA),
        "ln1_g": 1.0 + nrm(ks[15], (DEPTH, D_MODEL), 0.02),
        "ln1_b": nrm(ks[16], (DEPTH, D_MODEL), 0.02),
        "ln2_g": 1.0 + nrm(ks[17], (DEPTH, D_MODEL), 0.02),
        "ln2_b": nrm(ks[18], (DEPTH, D_MODEL), 0.02),
    }


def reference(x, w_in_even, w_out_even, w_in_odd, w_out_odd, lam_q1, lam_k1, lam_q2, lam_k2,
              subln_g, rel_bias, w_up, conv_w, conv_b, w_down, ln1_g, ln1_b, ln2_g, ln2_b):
    for l in range(DEPTH):
        i = l // 2
        if l % 2 == 0:
            m = even_mixer(x, w_in_even[i], w_out_even[i])
        else:
            m = odd_mixer(x, w_in_odd[i], w_out_odd[i], lam_q1[i], lam_k1[i], lam_q2[i], lam_k2[i],
                          subln_g[i], rel_bias, l)
        x = layer_norm(ALPHA * x + m, ln1_g[l], ln1_b[l])
        f = conv_ffn(x, w_up[l], conv_w[l], conv_b[l], w_down[l])
        x = layer_norm(ALPHA * x + f, ln2_g[l], ln2_b[l])
    return x
```

```python
import math, contextlib
import numpy as np
import concourse.bass as bass
import concourse.mybir as mybir
from concourse.bass_utils import run_bass_kernel_spmd

F32 = mybir.dt.float32
BF16 = mybir.dt.bfloat16
AF = mybir.ActivationFunctionType
ALU = mybir.AluOpType

D = 1024; SEQ = 2048; NT = 16; DFF = 2816; NP = 22; DEPTH = 4
ALPHA = (2 * DEPTH) ** 0.25
EPS = 1e-5
NEG = -30000.0
FL_EVEN = 112640; FL_ODD = 100352
PCH = 2048
import os
RSTOP = int(os.environ.get('RSTOP', '99'))
VG = int(os.environ.get('VG', '2'))
DSTOP = int(os.environ.get('DSTOP', '99'))


class Buf:
    __slots__ = ("name", "lw", "rd", "rdd", "excl")

    def __init__(self, name="", excl=False):
        self.name = name; self.lw = None; self.rd = {}; self.rdd = []; self.excl = excl


class Op:
    __slots__ = ("eng", "fn", "deps", "dma", "signal", "sem", "val", "guard")

    def __init__(self, eng, fn, deps, dma):
        self.eng = eng; self.fn = fn; self.deps = deps; self.dma = dma
        self.signal = False; self.sem = None; self.val = 0; self.guard = None


class Sched:
    def __init__(self, nc, es, n_dma_sems=32, same_engine_sync=True):
        self.nc = nc; self.ops = []; self.same = same_engine_sync
        self.h = {"pe": nc.tensor, "act": nc.scalar, "dve": nc.vector, "pool": nc.gpsimd, "sp": nc.sync}
        self.esem = {e: es.enter_context(nc.semaphore("sem_" + e)) for e in ["pe", "act", "dve", "pool"]}
        self.dsems = [es.enter_context(nc.semaphore("dsem%d" % i)) for i in range(n_dma_sems)]

    def add(self, eng, fn, reads=(), writes=(), dma=False):
        i = len(self.ops)
        deps = set()
        for b in reads:
            if b.lw is not None:
                deps.add(b.lw)
            if b.excl:
                for e2, o2 in b.rd.items():
                    if e2 != eng:
                        deps.add(o2)
        for b in writes:
            if b.lw is not None:
                deps.add(b.lw)
            deps.update(b.rd.values()); deps.update(b.rdd)
        deps.discard(i)
        for b in reads:
            if dma:
                b.rdd.append(i)
            else:
                b.rd[eng] = i
        for b in writes:
            b.lw = i; b.rd = {}; b.rdd = []
        self.ops.append(Op(eng, fn, deps, dma))
        return i

    def _skip(self, dop, op):
        if dop.fn is None:
            assert dop.eng == op.eng, "cross-engine dep on barrier"
            return True
        if (not dop.dma) and (not op.dma) and dop.eng == op.eng:
            if dop.eng == "pe" or not self.same:
                return True
        return False

    def emit(self):
        ops = self.ops
        for op in ops:
            for d in op.deps:
                if not self._skip(ops[d], op):
                    ops[d].signal = True
        for op in ops:
            if op.dma and op.fn is not None:
                op.signal = True
        cnt = {e: 0 for e in self.esem}
        dval = [0] * len(self.dsems); rr = 0
        for op in ops:
            if not op.signal:
                continue
            if op.dma:
                op.sem = self.dsems[rr]; op.guard = (self.dsems[rr], dval[rr])
                dval[rr] += 16; op.val = dval[rr]; rr = (rr + 1) % len(self.dsems)
            else:
                cnt[op.eng] += 1; op.sem = self.esem[op.eng]; op.val = cnt[op.eng]
        waited = {e: {} for e in self.h}
        nw = 0
        for op in ops:
            e = self.h[op.eng]
            need = {}
            for d in op.deps:
                dop = ops[d]
                if self._skip(dop, op):
                    continue
                k = id(dop.sem)
                if k not in need or need[k][1] < dop.val:
                    need[k] = (dop.sem, dop.val)
            if op.dma and op.signal and op.guard[1] > 0:
                k = id(op.guard[0])
                if k not in need or need[k][1] < op.guard[1]:
                    need[k] = op.guard
            w = waited[op.eng]
            for k, (sem, val) in need.items():
                if w.get(k, 0) < val:
                    e.wait_ge(sem, val); w[k] = val; nw += 1
            if op.fn is None:
                continue
            inst = op.fn()
            if op.signal:
                inst.then_inc(op.sem, 16 if op.dma else 1)
        return dict(n_ops=len(ops), n_waits=nw, cnt=cnt, dval=max(dval))


def _tile_k(w):
    K = w.shape[0] // 128
    return np.ascontiguousarray(w.reshape(K, 128, -1).transpose(1, 0, 2)).reshape(128, -1)


def _layer_weights(l, inp):
    parts = []
    i = l // 2
    if l % 2 == 0:
        win = inp["w_in_even"][i]; wout = inp["w_out_even"][i]
        for h in range(4):
            q = win[:, h * 128:(h + 1) * 128]; qs = np.concatenate([q[:, 64:], q[:, :64]], 1)
            k = win[:, 512 + h * 128:512 + (h + 1) * 128]; ks = np.concatenate([k[:, 64:], k[:, :64]], 1)
            v = win[:, 1024 + h * 128:1024 + (h + 1) * 128]; g = win[:, 1536 + h * 128:1536 + (h + 1) * 128]
            for pair in ([q, qs], [k, ks], [v, g]):
                parts.append(_tile_k(np.concatenate(pair, 1)))
        for u in range(4):
            q = win[:, 2048 + u * 128:2048 + (u + 1) * 128]; k = win[:, 2560 + u * 128:2560 + (u + 1) * 128]
            v = win[:, 3072 + u * 128:3072 + (u + 1) * 128]
            parts.append(_tile_k(np.concatenate([q, k], 1))); parts.append(_tile_k(v))
    else:
        win = inp["w_in_odd"][i]; wout = inp["w_out_odd"][i]
        for h in range(8):
            q = win[:, h * 128:(h + 1) * 128]; k = win[:, 1024 + h * 128:1024 + (h + 1) * 128]
            v = win[:, 2048 + h * 128:2048 + (h + 1) * 128]
            parts.append(_tile_k(np.concatenate([q, k], 1))); parts.append(_tile_k(v))
    parts.append(_tile_k(wout))
    wup = inp["w_up"][l]
    for p in range(NP):
        parts.append(_tile_k(np.concatenate([wup[:, p * 128:(p + 1) * 128], wup[:, DFF + p * 128:DFF + (p + 1) * 128]], 1)))
    parts.append(_tile_k(inp["w_down"][l]))
    out = np.ascontiguousarray(np.concatenate(parts, 1), dtype=np.float32)
    assert out.shape[1] == (FL_EVEN if l % 2 == 0 else FL_ODD)
    return out


def _offsets(l):
    if l % 2 == 0:
        return dict(ret=lambda h, j: h * 6144 + j * 2048, qk=lambda u: 24576 + u * 3072, v=lambda u: 24576 + u * 3072 + 2048,
                    wout=36864, up=lambda p: 45056 + p * 2048, down=90112)
    return dict(qk=lambda u: u * 3072, v=lambda u: u * 3072 + 2048, wout=24576, up=lambda p: 32768 + p * 2048, down=77824)


CST_DEC = 0; CST_QDEC = 512; CST_KDEC = 1024; CST_SBM = 1028; CST_U = 1156; CST_ID = 1284; CST_W = 1412


def _consts():
    c = np.zeros((128, CST_W), np.float32)
    lg = np.log(1.0 - 2.0 ** (-5.0 - np.arange(4, dtype=np.float64)))
    idx = np.arange(128, dtype=np.float64)
    for h in range(4):
        rel = idx[None, :] - idx[:, None]
        dec = np.where(rel >= 0, np.exp(lg[h] * np.maximum(rel, 0.0)), 0.0) / math.sqrt(128.0)
        c[:, CST_DEC + h * 128:CST_DEC + (h + 1) * 128] = dec
        c[:, CST_QDEC + h * 128:CST_QDEC + (h + 1) * 128] = np.exp(lg[h] * (idx + 1.0))[None, :]
        c[:, CST_KDEC + h] = np.exp(lg[h] * (127.0 - idx)) / math.sqrt(128.0)
    s_i = idx[:, None]; i_i = idx[None, :]
    c[:, CST_SBM:CST_SBM + 128] = np.where(s_i < i_i, 0.0, NEG)
    c[:, CST_U:CST_U + 128] = (idx[:, None] > idx[None, :]).astype(np.float32)
    c[:, CST_ID:CST_ID + 128] = np.eye(128)
    chunk_g = [float(np.exp(lg[h] * 128.0)) for h in range(4)]
    return c, chunk_g


def _rope():
    inv = (10000.0 ** (-np.arange(0, 128, 2, dtype=np.float32) / np.float32(128))).astype(np.float32)
    ang = np.arange(SEQ, dtype=np.float32)[:, None] * inv[None, :]
    cos = np.cos(ang).T.astype(np.float32); sin = np.sin(ang).T.astype(np.float32)
    r = np.zeros((2, 128, SEQ), np.float32)
    r[0, :64] = cos; r[0, 64:] = cos
    r[1, :64] = -sin; r[1, 64:] = sin
    return r


def _t5_bucket(rel):
    n = np.maximum(rel, 0)
    nf = np.maximum(n, 1).astype(np.float32)
    large = 16 + (np.log(nf / np.float32(16)) / np.float32(math.log(128 / 16)) * np.float32(16)).astype(np.int32)
    large = np.minimum(large, 31)
    return np.where(n < 16, n, large)


def _bias_tiles(rel_bias):
    s = np.arange(128)[:, None]; i = np.arange(128)[None, :]
    b0 = _t5_bucket(i - s); b1 = _t5_bucket(128 + i - s)
    bt = np.zeros((128, 8, 2, 128), np.float32)
    for h in range(8):
        bt[:, h, 0, :] = np.where(i >= s, rel_bias[b0, h], NEG)
        bt[:, h, 1, :] = rel_bias[b1, h]
    c31 = np.broadcast_to(rel_bias[31][None, :], (128, 8))
    return np.ascontiguousarray(np.concatenate([bt.reshape(128, -1), c31], 1), dtype=np.float32)


def build(layers=(0, 1, 2, 3), nseq=2, same_sync=True, dbg=99):
    nc = bass.Bass("TRN2", target_bir_lowering=False, dynamic_dma_scratch_size=256)
    cst_np, chunk_g = _consts()
    x_d = nc.dram_tensor("x", [2, SEQ, D], F32, kind="ExternalInput").ap()
    out_d = nc.dram_tensor("out", [2, SEQ, D], F32, kind="ExternalOutput").ap()
    xres = nc.dram_tensor("xres", [2, SEQ, D], F32, kind="Internal").ap()
    wl = {}; wb = {}
    for l in layers:
        FL = FL_EVEN if l % 2 == 0 else FL_ODD
        wl[l] = nc.dram_tensor("wl%d" % l, [128, FL], F32, kind="ExternalInput").ap()
        wb[l] = nc.dram_tensor("wb%d" % l, [128, FL], BF16, kind="Internal").ap()
    cst_d = nc.dram_tensor("cst", [128, CST_W], F32, kind="ExternalInput").ap()
    rope_d = nc.dram_tensor("rope", [2, 128, SEQ], F32, kind="ExternalInput").ap()
    bt_d = nc.dram_tensor("bt", [128, 2056], F32, kind="ExternalInput").ap()
    convp_d = nc.dram_tensor("convp", [128, DEPTH * NP * 8], F32, kind="ExternalInput").ap()
    lnbc_d = nc.dram_tensor("lnbc", [128, 16, D], F32, kind="ExternalInput").ap()
    lamp_d = nc.dram_tensor("lamp", [128, 2 * 4 * 64], F32, kind="ExternalInput").ap()
    subg_d = nc.dram_tensor("subg", [128, 2], F32, kind="ExternalInput").ap()

    es = contextlib.ExitStack()
    with es:
        S = Sched(nc, es, same_engine_sync=same_sync)
        A = S.add

        def sb(n, shp, dt=F32):
            return es.enter_context(nc.sbuf_tensor("s_" + n, shp, dt))

        ps = es.enter_context(nc.psum_tensor("ps", [128, 8, 512], F32))
        pb = [Buf("pb%d" % i, excl=True) for i in range(8)]
        psb = [ps[:, i, :].bitcast(BF16) for i in range(8)]

        xT = sb("xT", [128, 8, SEQ], BF16); xTb = [Buf("xT%d" % t) for t in range(NT)]
        ya = sb("ya", [128, 8, SEQ], BF16); yab = [Buf("ya%d" % c) for c in range(8)]
        ya_flat = ya[:].rearrange("p k t -> p (k t)")
        wbig = sb("wbig", [128, 22528], BF16); wbigb = Buf("wbig")
        NSLOT = 4
        wsl = [sb("wsl%d" % i, [128, 2048], BF16) for i in range(NSLOT)]; wslb = [Buf("wsl%d" % i) for i in range(NSLOT)]
        slot_rr = [0]
        cst = sb("cst", [128, CST_W]); cstb = Buf("cst")
        cbf = sb("cbf", [128, 384], BF16); cbfb = Buf("cbf")
        U_bf = cbf[:, 0:128]; ones_bf = cbf[:, 128:256]; id_bf = cbf[:, 256:384]
        id_f = cst[:, CST_ID:CST_ID + 128]
        btt = sb("btt", [128, 2056]); bttb = Buf("btt")
        convp = sb("convp", [128, DEPTH * NP * 8]); convpb = Buf("convp")
        lnt = sb("lnt", [128, 2, D]); lntb = Buf("lnt")
        lamp = sb("lamp", [128, 512]); lampb = Buf("lamp")
        subg = sb("subg", [128, 2]); subgb = Buf("subg")
        lamt = sb("lamt", [128, 8]); lamtb = Buf("lamt")
        mhalf = sb("mhalf", [128, 1]); mhalfb = Buf("mhalf")
        vsb = [sb("vsb%d" % i, [128, 16, 130], BF16) for i in range(2)]; vsbb = [Buf("vsb%d" % i) for i in range(2)]
        halo = sb("halo", [128, NP, 2, 2]); halob = [Buf("halo%d" % p) for p in range(NP)]
        NST = 8
        stt = [sb("stt%d" % i, [128, 16]) for i in range(NST)]; sttb = [Buf("stt%d" % i) for i in range(NST)]
        st_rr = [0]
        rstat = sb("rstat", [128, 16, 8]); rstatb = Buf("rstat")
        bnst = sb("bnst", [128, 16, 6]); bnstb = Buf("bnst")
        NAR = 60
        arena = sb("arena", [128, NAR * 256]); arb = [Buf("ar%d" % i) for i in range(NAR)]

        def at(u0, nun, dt=F32):
            a = arena[:, u0 * 256:(u0 + nun) * 256]
            if dt == BF16:
                a = a.bitcast(BF16)
            return a, arb[u0:u0 + nun]

        def stat():
            i = st_rr[0]; st_rr[0] = (i + 1) % NST
            return stt[i], sttb[i]

        def wb_bufs(l, off, ln):
            return wbb[l][off // PCH:(off + ln - 1) // PCH + 1]

        def load_w(l, off, ln, K):
            i = slot_rr[0]; slot_rr[0] = (i + 1) % NSLOT
            dst = wsl[i][:, 0:ln]
            A("sp", lambda: nc.sync.dma_start(out=dst, in_=wb[l][:, off:off + ln]), reads=wb_bufs(l, off, ln), writes=[wslb[i]], dma=True)
            return wsl[i][:, 0:ln].rearrange("p (k c) -> p k c", k=K), wslb[i]

        A("sp", lambda: nc.sync.dma_start(out=cst[:], in_=cst_d[:, :]), writes=[cstb], dma=True)
        A("sp", lambda: nc.sync.dma_start(out=btt[:], in_=bt_d[:, :]), writes=[bttb], dma=True)
        A("sp", lambda: nc.sync.dma_start(out=convp[:], in_=convp_d[:, :]), writes=[convpb], dma=True)
        A("sp", lambda: nc.sync.dma_start(out=lamp[:], in_=lamp_d[:, :]), writes=[lampb], dma=True)
        A("sp", lambda: nc.sync.dma_start(out=subg[:], in_=subg_d[:, :]), writes=[subgb], dma=True)
        A("dve", lambda: nc.vector.tensor_copy(out=cbf[:, 0:128], in_=cst[:, CST_U:CST_U + 128]), reads=[cstb], writes=[cbfb])
        A("dve", lambda: nc.vector.memset(cbf[:, 128:256], 1.0), writes=[cbfb])
        A("dve", lambda: nc.vector.tensor_copy(out=cbf[:, 256:384], in_=cst[:, CST_ID:CST_ID + 128]), reads=[cstb], writes=[cbfb])
        A("dve", lambda: nc.vector.memset(mhalf[:], -0.5), writes=[mhalfb])
        for i in range(2):
            A("dve", (lambda i=i: nc.vector.memset(vsb[i][:, :, 128:130], 1.0)), writes=[vsbb[i]])
        for l in layers:
            if l % 2 == 1:
                i = l // 2
                lam_init = 0.8 - 0.6 * math.exp(-0.3 * l)
                t, tb_ = stat()
                pr = sb("lamprod%d" % i, [128, 128])
                prb = Buf()
                base = i * 256
                A("dve", (lambda base=base, pr=pr: nc.vector.tensor_tensor(out=pr[:, 0:64], in0=lamp[:, base:base + 64], in1=lamp[:, base + 64:base + 128], op=ALU.mult)), reads=[lampb], writes=[prb])
                A("dve", (lambda base=base, pr=pr: nc.vector.tensor_tensor(out=pr[:, 64:128], in0=lamp[:, base + 128:base + 192], in1=lamp[:, base + 192:base + 256], op=ALU.mult)), reads=[lampb, prb], writes=[prb])
                A("dve", (lambda pr=pr, t=t: nc.vector.reduce_sum(out=t[:, 0:2], in_=pr[:].rearrange("p (a b) -> p a b", a=2), axis=mybir.AxisListType.X)), reads=[prb], writes=[tb_])
                A("act", (lambda t=t: nc.scalar.activation(out=t[:, 2:4], in_=t[:, 0:2], func=AF.Exp)), reads=[tb_], writes=[tb_])
                A("dve", (lambda t=t, i=i, li=lam_init: nc.vector.scalar_tensor_tensor(out=lamt[:, i:i + 1], in0=t[:, 3:4], scalar=-li, in1=t[:, 2:3], op0=ALU.add, op1=ALU.subtract)), reads=[tb_], writes=[lamtb])

        wbb = {}
        st32 = [wbig[:, i * 4096:(i + 1) * 4096].bitcast(F32) for i in range(3)]
        st16 = [wbig[:, 12288 + i * 2048:12288 + (i + 1) * 2048] for i in range(3)]
        st32b = [Buf() for _ in range(3)]; st16b = [Buf() for _ in range(3)]
        ci = 0
        for l in layers:
            FL = FL_EVEN if l % 2 == 0 else FL_ODD
            wbb[l] = [Buf("wb%d_%d" % (l, c)) for c in range(FL // PCH)]
            for c in range(FL // PCH):
                j = ci % 3; ci += 1
                A("sp", (lambda l=l, c=c, j=j: nc.sync.dma_start(out=st32[j], in_=wl[l][:, c * PCH:(c + 1) * PCH])), writes=[st32b[j]], dma=True)
                if ci % 2 == 0:
                    A("dve", (lambda j=j: nc.vector.tensor_copy(out=st16[j], in_=st32[j])), reads=[st32b[j]], writes=[st16b[j]])
                else:
                    A("pool", (lambda j=j: nc.gpsimd.tensor_copy(out=st16[j], in_=st32[j])), reads=[st32b[j]], writes=[st16b[j]])
                A("sp", (lambda l=l, c=c, j=j: nc.sync.dma_start(out=wb[l][:, c * PCH:(c + 1) * PCH], in_=st16[j])), reads=[st16b[j]], writes=[wbb[l][c]], dma=True)
        A("sp", None, reads=[], writes=st32b + st16b + [wbigb])

        evac_rr = [0]

        def evac_copy(out, in_, reads, writes, eng=None):
            if eng is None:
                eng = "act" if evac_rr[0] % 2 == 0 else "dve"; evac_rr[0] += 1
            if eng == "act":
                A("act", lambda: nc.scalar.activation(out=out, in_=in_, func=AF.Copy), reads=reads, writes=writes)
            else:
                A("dve", lambda: nc.vector.tensor_copy(out=out, in_=in_), reads=reads, writes=writes)

        def proj_fm(wt, wtb, col0, bank, tb):
            for k in range(8):
                A("pe", (lambda k=k: nc.tensor.matmul(ps[:, bank, :], lhsT=wt[:, k, col0:col0 + 128], rhs=xT[:, k, tb * 512:(tb + 1) * 512], start=(k == 0), stop=(k == 7))),
                  reads=[wtb] + xTb[tb * 4:tb * 4 + 4], writes=[pb[bank]])

        def proj_tm(wt, wtb, col0, ncol, bank, boff, t):
            for k in range(8):
                A("pe", (lambda k=k: nc.tensor.matmul(ps[:, bank, boff:boff + ncol], lhsT=xT[:, k, t * 128:(t + 1) * 128], rhs=wt[:, k, col0:col0 + ncol], start=(k == 0), stop=(k == 7))),
                  reads=[wtb, xTb[t]], writes=[pb[bank]])

        ln_tiles = {}
        for nm, u0 in (("xt", 0), ("z", 8), ("zn", 16), ("xn", 24)):
            for j in range(2):
                ln_tiles[(nm, j)] = at(u0 + 4 * j, 4)

        def ln_stage_a(t, lhs_fn, lhs_bufs, K, wv, src, srcb):
            j = t % 2
            xt, xtb = ln_tiles[("xt", j)]; z, zb = ln_tiles[("z", j)]
            for half in range(2):
                for k in range(K):
                    A("pe", (lambda half=half, k=k: nc.tensor.matmul(ps[:, 4 + half, :], lhsT=lhs_fn(k), rhs=wv[:, k, half * 512:(half + 1) * 512], start=(k == 0), stop=(k == K - 1))),
                      reads=lhs_bufs + [wbigb], writes=[pb[4 + half]])
            A("sp", lambda: nc.sync.dma_start(out=xt, in_=src), reads=srcb, writes=xtb, dma=True)
            pz = ps[:, 4:6, :].rearrange("p a b -> p (a b)")
            A("dve", lambda: nc.vector.scalar_tensor_tensor(out=z, in0=xt, scalar=ALPHA, in1=pz, op0=ALU.mult, op1=ALU.add), reads=xtb + [pb[4], pb[5]], writes=zb)
            st, stb = stat()
            for half in range(2):
                A("dve", (lambda half=half: nc.vector.bn_stats(out=bnst[:, half, :], in_=z[:, half * 512:(half + 1) * 512])), reads=zb, writes=[bnstb])
            A("dve", lambda: nc.vector.bn_aggr(out=st[:, 0:2], in_=bnst[:, 0:2, :].rearrange("p a b -> p (a b)")), reads=[bnstb], writes=[stb])
            A("dve", lambda: nc.vector.tensor_scalar(out=st[:, 2:3], in0=st[:, 1:2], scalar1=EPS, scalar2=None, op0=ALU.add), reads=[stb], writes=[stb])
            A("pool", lambda: nc.gpsimd.tensor_tensor(out=st[:, 3:4], in0=st[:, 2:3], in1=mhalf[:], op=ALU.pow), reads=[stb, mhalfb], writes=[stb])
            A("dve", lambda: nc.vector.scalar_tensor_tensor(out=st[:, 4:5], in0=st[:, 0:1], scalar=-1.0, in1=st[:, 3:4], op0=ALU.mult, op1=ALU.mult), reads=[stb], writes=[stb])
            return (t, j, st, stb)

        def ln_stage_b(state, dst, dstb, do_transpose):
            t, j, st, stb = state
            z, zb = ln_tiles[("z", j)]; zn, znb = ln_tiles[("zn", j)]; xn, xnb = ln_tiles[("xn", j)]
            A("act", lambda: nc.scalar.activation(out=zn, in_=z, func=AF.Identity, scale=st[:, 3:4], bias=st[:, 4:5]), reads=zb + [stb], writes=znb)
            A("dve", lambda: nc.vector.tensor_tensor(out=zn, in0=zn, in1=lnt[:, 0, :], op=ALU.mult), reads=znb + [lntb], writes=znb)
            A("pool", lambda: nc.gpsimd.tensor_tensor(out=xn, in0=zn, in1=lnt[:, 1, :], op=ALU.add), reads=znb + [lntb], writes=xnb)
            A("sp", lambda: nc.sync.dma_start(out=dst, in_=xn), reads=xnb, writes=dstb, dma=True)
            if do_transpose:
                for k in range(8):
                    A("pe", (lambda k=k: nc.tensor.transpose(ps[:, 6 + k // 4, (k % 4) * 128:(k % 4 + 1) * 128], xn[:, k * 128:(k + 1) * 128], id_f)),
                      reads=xnb + [cstb], writes=[pb[6 + k // 4]])
                ptr = ps[:, 6:8, :].rearrange("p a (b c) -> p (a b) c", c=128)
                A("act", lambda: nc.scalar.activation(out=xT[:, :, t * 128:(t + 1) * 128], in_=ptr, func=AF.Copy), reads=[pb[6], pb[7]], writes=[xTb[t]])

        def ln_phase(tiles, lhs_fn_t, lhs_bufs, K, wv, src_fn, dst_fn, do_transpose):
            pend = None
            for t in tiles:
                src, srcb = src_fn(t)
                stt_ = ln_stage_a(t, (lambda k, t=t: lhs_fn_t(k, t)), lhs_bufs, K, wv, src, srcb)
                if pend is not None:
                    d, db = dst_fn(pend[0]); ln_stage_b(pend, d, db, do_transpose)
                pend = stt_
            d, db = dst_fn(pend[0]); ln_stage_b(pend, d, db, do_transpose)

        def load_ln(l, which):
            A("sp", lambda: nc.sync.dma_start(out=lnt[:], in_=lnbc_d[:, l * 4 + which * 2:l * 4 + which * 2 + 2, :]), writes=[lntb], dma=True)

        def attn_proj(l, u, uidx):
            off = _offsets(l)
            qT, qTb = at((uidx % 2) * 8, 4, BF16); kT, kTb = at((uidx % 2) * 8 + 4, 4, BF16)
            wqk, wqkb = load_w(l, off["qk"](u), 2048, 8)
            wv_, wvb = load_w(l, off["v"](u), 1024, 8)
            bank = 0
            for (dst, dstb, col0) in ((qT, qTb, 0), (kT, kTb, 128)):
                for tb in range(4):
                    b = bank % 3; bank += 1
                    proj_fm(wqk, wqkb, col0, b, tb)
                    evac_copy(dst[:, tb * 512:(tb + 1) * 512], ps[:, b, :], [pb[b]], dstb)
            vi = uidx % 2
            for t4 in range(4):
                b = bank % 3; bank += 1
                for tt in range(4):
                    proj_tm(wv_, wvb, 0, 128, b, tt * 128, t4 * 4 + tt)
                evac_copy(vsb[vi][:, t4 * 4:t4 * 4 + 4, 0:128], ps[:, b, :].rearrange("p (a c) -> p a c", c=128), [pb[b]], [vsbb[vi]])
            return qT, qTb, kT, kTb, vsb[vi], vsbb[vi]

        def diff_unit(l, h, uidx):
            i_odd = l // 2
            lam_init = 0.8 - 0.6 * math.exp(-0.3 * l)
            csc = (1.0 - lam_init) ** -2
            qT, qTb, kT, kTb, v, vb = attn_proj(l, h, uidx)
            if DSTOP < 2: return
            PT = [at(16 + i, 1, BF16) for i in range(3)]
            TMP = [at(19 + i, 1) for i in range(4)]
            o0, o0b = at(23, 2)
            OW = [at(25 + i, 1) for i in range(4)]
            ONB = [at(29 + i, 1, BF16) for i in range(2)]
            pt_rr = 0; tmp_rr = 0; s_rr = 0; ow_rr = 0; on_rr = 0
            for g in range(4):
                for m in range(2):
                    lo = m * 64
                    steps = list(range(0, 4 * g + 4))
                    sbank = {}

                    def emit_S(kb, g=g, lo=lo):
                        nonlocal s_rr
                        r = kb - 4 * g; c0 = max(r, 0) * 128; N = 512 - c0
                        b = s_rr % 3; s_rr += 1; sbank[kb] = b
                        A("pe", lambda: nc.tensor.matmul(ps[:, b, 0:N], lhsT=kT[lo:lo + 64, kb * 128:(kb + 1) * 128], rhs=qT[lo:lo + 64, g * 512 + c0:(g + 1) * 512], start=True, stop=True),
                          reads=kTb + qTb, writes=[pb[b]])
                    emit_S(steps[0])
                    for si, kb in enumerate(steps):
                        if si + 1 < len(steps):
                            emit_S(steps[si + 1])
                        r = kb - 4 * g; c0 = max(r, 0) * 128; N = 512 - c0
                        b = sbank[kb]
                        pt, ptb = PT[pt_rr % 3]; pt_rr += 1
                        j0 = max(r, 0)
                        far0 = None
                        for j in range(j0, 4):
                            d = 4 * g + j - kb
                            lc = j * 128 - c0
                            if d <= 1:
                                tm, tmb = TMP[tmp_rr % 4]; tmp_rr += 1
                                tmv = tm[:, 0:128]
                                bto = h * 256 + d * 128
                                A("dve", (lambda lc=lc, tmv=tmv, bto=bto, b=b: nc.vector.scalar_tensor_tensor(out=tmv, in0=ps[:, b, lc:lc + 128], scalar=0.125, in1=btt[:, bto:bto + 128], op0=ALU.mult, op1=ALU.add)),
                                  reads=[pb[b], bttb], writes=tmb)
                                A("act", (lambda lc=lc, tmv=tmv, pt=pt: nc.scalar.activation(out=pt[:, lc:lc + 128], in_=tmv, func=AF.Exp)), reads=tmb, writes=ptb)
                            elif far0 is None:
                                far0 = lc
                        if far0 is not None:
                            A("act", (lambda far0=far0, N=N, pt=pt, b=b: nc.scalar.activation(out=pt[:, far0:N], in_=ps[:, b, far0:N], func=AF.Exp, scale=0.125, bias=btt[:, 2048 + h:2049 + h])),
                              reads=[pb[b], bttb], writes=ptb)
                        for j in range(j0, 4):
                            lc = j * 128 - c0
                            A("pe", (lambda j=j, lc=lc, pt=pt, kb=kb, g=g: nc.tensor.matmul(ps[:, 3 + j, 0:129], lhsT=pt[:, lc:lc + 128], rhs=v[:, kb, 0:129], start=(kb == 0), stop=(kb == 4 * g + j))),
                              reads=ptb + [vb], writes=[pb[3 + j]])
                    for j in range(4 if DSTOP >= 3 else 0):
                        st, stb = stat()
                        A("dve", (lambda j=j, st=st: nc.vector.reciprocal(out=st[:, 0:1], in_=ps[:, 3 + j, 128:129])), reads=[pb[3 + j]], writes=[stb])
                        if m == 0:
                            A("act", (lambda j=j, st=st: nc.scalar.activation(out=o0[:, j * 128:(j + 1) * 128], in_=ps[:, 3 + j, 0:128], func=AF.Identity, scale=st[:, 0:1])), reads=[pb[3 + j], stb], writes=o0b)
                        else:
                            ow, owb = OW[ow_rr % 4]; ow_rr += 1
                            o1 = ow[:, 0:128]; oo = ow[:, 128:256]
                            A("act", (lambda j=j, st=st, o1=o1: nc.scalar.activation(out=o1, in_=ps[:, 3 + j, 0:128], func=AF.Identity, scale=st[:, 0:1])), reads=[pb[3 + j], stb], writes=owb)
                            A("dve", (lambda j=j, o1=o1, oo=oo: nc.vector.scalar_tensor_tensor(out=oo, in0=o1, scalar=lamt[:, i_odd:i_odd + 1], in1=o0[:, j * 128:(j + 1) * 128], op0=ALU.mult, op1=ALU.add)),
                              reads=owb + o0b + [lamtb], writes=owb)
                            A("act", (lambda st=st, o1=o1, oo=oo: nc.scalar.activation(out=o1, in_=oo, func=AF.Square, accum_out=st[:, 1:2])), reads=owb, writes=owb + [stb])
                            A("dve", (lambda st=st: nc.vector.tensor_scalar(out=st[:, 2:3], in0=st[:, 1:2], scalar1=csc / 128.0, scalar2=EPS * csc, op0=ALU.mult, op1=ALU.add)), reads=[stb], writes=[stb])
                            A("pool", (lambda st=st: nc.gpsimd.tensor_tensor(out=st[:, 3:4], in0=st[:, 2:3], in1=mhalf[:], op=ALU.pow)), reads=[stb, mhalfb], writes=[stb])
                            onb, onbb = ONB[on_rr % 2]; on_rr += 1
                            onv = onb[:, 0:128]
                            A("act", (lambda st=st, oo=oo, onv=onv: nc.scalar.activation(out=onv, in_=oo, func=AF.Identity, scale=st[:, 3:4])), reads=owb + [stb], writes=onbb)
                            A("pe", (lambda j=j, onv=onv: nc.tensor.transpose(psb[7][:, j * 128:(j + 1) * 128], onv, id_bf)), reads=onbb + [cbfb], writes=[pb[7]])
                    if m == 1 and DSTOP >= 3:
                        A("act", lambda g=g: nc.scalar.activation(out=ya[:, h, g * 512:(g + 1) * 512], in_=psb[7][:, 0:512], func=AF.Identity, scale=subg[:, i_odd:i_odd + 1]), reads=[pb[7], subgb], writes=[yab[h]])

        def sb_unit(l, u, uidx):
            qT, qTb, kT, kTb, v, vb = attn_proj(l, u, uidx)
            AT = [at(16 + i, 1, BF16) for i in range(3)]
            LB = [at(19 + i, 1, BF16) for i in range(2)]
            E = [at(21 + 2 * i, 2) for i in range(2)]
            TT = [at(25 + 2 * i, 2) for i in range(2)]
            ZD = [at(29 + i, 1) for i in range(2)]
            z_rr = 0; c_rr = 0; a_rr = 0; l_rr = 0; e_rr = 0; t_rr = 0; zd_rr = 0; o_rr = 0
            sbm = cst[:, CST_SBM:CST_SBM + 128]
            for g in range(4):
                for a in range(2):
                    lo = a * 64
                    ob = 6 + (o_rr % 2); o_rr += 1
                    steps = list(range(4 * g + 3, -1, -1))
                    zbank = {}

                    def emit_Z(kb, g=g, lo=lo):
                        nonlocal z_rr
                        r = kb - 4 * g; c0 = max(r, 0) * 128; N = 512 - c0
                        b = z_rr % 3; z_rr += 1; zbank[kb] = b
                        A("pe", lambda: nc.tensor.matmul(ps[:, b, 0:N], lhsT=kT[lo:lo + 64, kb * 128:(kb + 1) * 128], rhs=qT[lo:lo + 64, g * 512 + c0:(g + 1) * 512], start=True, stop=True),
                          reads=kTb + qTb, writes=[pb[b]])
                    emit_Z(steps[0])
                    for si, kb in enumerate(steps):
                        if si + 1 < len(steps):
                            emit_Z(steps[si + 1])
                        r = kb - 4 * g; c0 = max(r, 0) * 128; N = 512 - c0
                        zb_ = zbank[kb]
                        e, eb = E[e_rr % 2]; e_rr += 1
                        lb, lbb = LB[l_rr % 2]; l_rr += 1
                        tt, ttb = TT[t_rr % 2]; t_rr += 1
                        att, attb = AT[a_rr % 3]; a_rr += 1
                        cb = 3 + (c_rr % 2); c_rr += 1
                        nd = 0
                        if r >= 0:
                            zd, zdb = ZD[zd_rr % 2]; zd_rr += 1
                            zdv = zd[:, 0:128]
                            A("dve", (lambda zdv=zdv, zb_=zb_: nc.vector.scalar_tensor_tensor(out=zdv, in0=ps[:, zb_, 0:128], scalar=0.125, in1=sbm, op0=ALU.mult, op1=ALU.add)), reads=[pb[zb_], cstb], writes=zdb)
                            A("act", (lambda zdv=zdv, e=e: nc.scalar.activation(out=e[:, 0:128], in_=zdv, func=AF.Exp)), reads=zdb, writes=eb)
                            nd = 128
                        if N > nd:
                            A("act", (lambda nd=nd, N=N, e=e, zb_=zb_: nc.scalar.activation(out=e[:, nd:N], in_=ps[:, zb_, nd:N], func=AF.Exp, scale=0.125)), reads=[pb[zb_]], writes=eb)
                        A("act", (lambda N=N, e=e, lb=lb: nc.scalar.activation(out=lb[:, 0:N], in_=e[:, 0:N], func=AF.Ln, bias=1.0)), reads=eb, writes=lbb)
                        A("pe", (lambda N=N, lb=lb, cb=cb: nc.tensor.matmul(ps[:, cb, 0:N], lhsT=U_bf, rhs=lb[:, 0:N], start=True, stop=True)), reads=lbb + [cbfb], writes=[pb[cb]])
                        if r >= 0:
                            A("dve", (lambda zdv=zdv, lb=lb, tt=tt: nc.vector.tensor_tensor(out=tt[:, 0:128], in0=zdv, in1=lb[:, 0:128], op=ALU.subtract)), reads=zdb + lbb, writes=ttb)
                        if N > nd:
                            A("dve", (lambda nd=nd, N=N, lb=lb, tt=tt, zb_=zb_: nc.vector.scalar_tensor_tensor(out=tt[:, nd:N], in0=ps[:, zb_, nd:N], scalar=0.125, in1=lb[:, nd:N], op0=ALU.mult, op1=ALU.subtract)),
                              reads=[pb[zb_]] + lbb, writes=ttb)
                        A("dve", (lambda N=N, tt=tt, cb=cb: nc.vector.scalar_tensor_tensor(out=tt[:, 0:N], in0=ps[:, cb, 0:N], scalar=-1.0, in1=tt[:, 0:N], op0=ALU.mult, op1=ALU.add)), reads=[pb[cb]] + ttb, writes=ttb)
                        if N > nd and kb != 4 * g + 3:
                            A("dve", (lambda nd=nd, N=N, tt=tt, c0=c0: nc.vector.scalar_tensor_tensor(out=tt[:, nd:N], in0=ps[:, 5, c0 + nd:512], scalar=-1.0, in1=tt[:, nd:N], op0=ALU.mult, op1=ALU.add)), reads=[pb[5]] + ttb, writes=ttb)
                        A("act", (lambda N=N, tt=tt, att=att: nc.scalar.activation(out=att[:, 0:N], in_=tt[:, 0:N], func=AF.Exp)), reads=ttb, writes=attb)
                        if kb >= 1:
                            A("pe", (lambda N=N, lb=lb, c0=c0, kb=kb, g=g: nc.tensor.matmul(ps[:, 5, c0:512], lhsT=ones_bf, rhs=lb[:, 0:N], start=(kb == 4 * g + 3), stop=(kb == 1))), reads=lbb + [cbfb], writes=[pb[5]])
                        A("pe", (lambda N=N, att=att, c0=c0, kb=kb, ob=ob, g=g, lo=lo: nc.tensor.matmul(ps[lo:lo + 64, ob, c0:512], lhsT=v[:, kb, lo:lo + 64], rhs=att[:, 0:N], start=(kb == 4 * g + 3), stop=(kb == 0))),
                          reads=attb + [vb], writes=[pb[ob]])
                    evac_copy(ya[lo:lo + 64, 4 + u, g * 512:(g + 1) * 512], ps[lo:lo + 64, ob, :], [pb[ob]], [yab[4 + u]])

        def ret_unit(l, h, uidx):
            off = _offsets(l)
            qT, qTb = at((uidx % 2) * 8, 4, BF16); kT, kTb = at((uidx % 2) * 8 + 4, 4, BF16)
            qd, qdb = at(16, 4, BF16); kd, kdb = at(20, 4, BF16)
            oraw, orawb = at(24, 8); sgt, sgtb = at(32, 4, BF16); ytok, ytokb = at(36, 4, BF16)
            SD = [at(40 + i, 1, BF16) for i in range(2)]
            stf, stfb = at(42, 1); stbf, stbfb = at(43, 1, BF16)
            R1 = [at(44 + 2 * i, 2) for i in range(2)]; R2 = [at(48 + 2 * i, 2) for i in range(2)]
            CS = [at(52 + 4 * i, 4) for i in range(2)]
            w_qq, w_qqb = load_w(l, off["ret"](h, 0), 2048, 8)
            w_kk, w_kkb = load_w(l, off["ret"](h, 1), 2048, 8)
            w_vg, w_vgb = load_w(l, off["ret"](h, 2), 2048, 8)
            vi = uidx % 2; v = vsb[vi]; vb = vsbb[vi]
            bank = 0
            for tb in range(4):
                cs, csb = CS[tb % 2]
                csv = cs.rearrange("p (a c) -> p a c", a=2)
                A("sp", (lambda tb=tb, csv=csv: nc.sync.dma_start(out=csv, in_=rope_d[:, :, tb * 512:(tb + 1) * 512].rearrange("a p c -> p a c"))), writes=csb, dma=True)
                for (w_, w_b, dst, dstb) in ((w_qq, w_qqb, qT, qTb), (w_kk, w_kkb, kT, kTb)):
                    b1 = bank % 4; bank += 1; b2 = bank % 4; bank += 1
                    proj_fm(w_, w_b, 0, b1, tb); proj_fm(w_, w_b, 128, b2, tb)
                    r1, r1b = R1[(bank // 2) % 2]; r2, r2b = R2[(bank // 2) % 2]
                    A("dve", (lambda b1=b1, r1=r1, csv=csv: nc.vector.tensor_tensor(out=r1, in0=ps[:, b1, :], in1=csv[:, 0, :], op=ALU.mult)), reads=[pb[b1]] + csb, writes=r1b)
                    A("dve", (lambda b2=b2, r2=r2, csv=csv: nc.vector.tensor_tensor(out=r2, in0=ps[:, b2, :], in1=csv[:, 1, :], op=ALU.mult)), reads=[pb[b2]] + csb, writes=r2b)
                    A("pool", (lambda r1=r1, r2=r2, dst=dst, tb=tb: nc.gpsimd.tensor_tensor(out=dst[:, tb * 512:(tb + 1) * 512], in0=r1, in1=r2, op=ALU.add)), reads=r1b + r2b, writes=dstb)
            if RSTOP < 2: return
            qdec = cst[:, CST_QDEC + h * 128:CST_QDEC + (h + 1) * 128]
            A("pool", lambda: nc.gpsimd.tensor_tensor(out=qd.rearrange("p (c i) -> p c i", i=128), in0=qT.rearrange("p (c i) -> p c i", i=128), in1=qdec.unsqueeze(1).broadcast_to([128, 16, 128]), op=ALU.mult),
              reads=qTb + [cstb], writes=qdb)
            if RSTOP < 3: return
            for c4 in range(4):
                b = bank % 4; bank += 1
                for cc in range(4):
                    c = c4 * 4 + cc
                    A("pe", (lambda c=c, cc=cc, b=b: nc.tensor.transpose(psb[b][:, cc * 128:(cc + 1) * 128], kT[:, c * 128:(c + 1) * 128], id_bf)), reads=kTb + [cbfb], writes=[pb[b]])
                A("act", (lambda c4=c4, b=b: nc.scalar.activation(out=kd[:, c4 * 512:(c4 + 1) * 512], in_=psb[b][:, 0:512], func=AF.Identity, scale=cst[:, CST_KDEC + h:CST_KDEC + h + 1])), reads=[pb[b], cstb], writes=kdb)
            if RSTOP < 4: return
            for t2 in range(8):
                b = bank % 4; bank += 1
                for tt in range(2):
                    proj_tm(w_vg, w_vgb, 0, 256, b, tt * 256, t2 * 2 + tt)
                pv = ps[:, b, :].rearrange("p (a c) -> p a c", c=256)
                if VG >= 1:
                    A("dve", (lambda t2=t2, pv=pv: nc.vector.tensor_copy(out=v[:, t2 * 2:t2 * 2 + 2, 0:128], in_=pv[:, :, 0:128])), reads=[pb[b]], writes=[vb])
                if VG >= 2:
                  A("act", (lambda t2=t2, pv=pv: nc.scalar.activation(out=sgt.rearrange("p (c e) -> p c e", e=128)[:, t2 * 2:t2 * 2 + 2, :], in_=pv[:, :, 128:256], func=(AF.Silu if VG == 2 else AF.Copy))), reads=[pb[b], vb], writes=sgtb)
            if RSTOP < 5: return
            decT = cst[:, CST_DEC + h * 128:CST_DEC + (h + 1) * 128]
            for c in range(16):
                sb_ = 4 + c % 2; ob = 6 if c % 2 == 0 else 2; spb = 7 if c % 2 == 0 else 3
                sd, sdb = SD[c % 2]; sdv = sd[:, 0:128]
                A("pe", (lambda c=c, sb_=sb_: nc.tensor.matmul(ps[:, sb_, 0:128], lhsT=kT[:, c * 128:(c + 1) * 128], rhs=qT[:, c * 128:(c + 1) * 128], start=True, stop=True)), reads=kTb + qTb, writes=[pb[sb_]])
                A("dve", (lambda sb_=sb_, sdv=sdv: nc.vector.tensor_tensor(out=sdv, in0=ps[:, sb_, 0:128], in1=decT, op=ALU.mult)), reads=[pb[sb_], cstb], writes=sdb)
                A("pe", (lambda c=c, sdv=sdv, ob=ob: nc.tensor.matmul(ps[:, ob, 0:128], lhsT=sdv, rhs=v[:, c, 0:128], start=True, stop=(c == 0))), reads=sdb + [vb], writes=[pb[ob]])
                if c > 0:
                    A("pe", (lambda c=c, ob=ob: nc.tensor.matmul(ps[:, ob, 0:128], lhsT=qd[:, c * 128:(c + 1) * 128], rhs=stbf[:, 0:128], start=False, stop=True)), reads=qdb + stbfb, writes=[pb[ob]])
                if c < 15:
                    A("pe", (lambda c=c, spb=spb: nc.tensor.matmul(ps[:, spb, 0:128], lhsT=kd[:, c * 128:(c + 1) * 128], rhs=v[:, c, 0:128], start=True, stop=True)), reads=kdb + [vb], writes=[pb[spb]])
                    if c == 0:
                        A("dve", (lambda spb=spb: nc.vector.tensor_copy(out=stf[:, 0:128], in_=ps[:, spb, 0:128])), reads=[pb[spb]], writes=stfb)
                    else:
                        A("dve", (lambda spb=spb: nc.vector.scalar_tensor_tensor(out=stf[:, 0:128], in0=stf[:, 0:128], scalar=chunk_g[h], in1=ps[:, spb, 0:128], op0=ALU.mult, op1=ALU.add)), reads=[pb[spb]] + stfb, writes=stfb)
                    A("pool", lambda: nc.gpsimd.tensor_copy(out=stbf[:, 0:128], in_=stf[:, 0:128]), reads=stfb, writes=stbfb)
                A("dve", (lambda c=c, ob=ob: nc.vector.bn_stats(out=bnst[:, c, :], in_=ps[:, ob, 0:128])), reads=[pb[ob]], writes=[bnstb])
                A("act", (lambda c=c, ob=ob: nc.scalar.activation(out=oraw[:, c * 128:(c + 1) * 128], in_=ps[:, ob, 0:128], func=AF.Copy)), reads=[pb[ob]], writes=orawb)
            if RSTOP < 6: return
            for c in range(16):
                A("dve", (lambda c=c: nc.vector.bn_aggr(out=rstat[:, c, 0:2], in_=bnst[:, c, :])), reads=[bnstb], writes=[rstatb])
            A("dve", lambda: nc.vector.tensor_scalar(out=rstat[:, :, 2:3], in0=rstat[:, :, 1:2], scalar1=EPS, scalar2=None, op0=ALU.add), reads=[rstatb], writes=[rstatb])
            A("pool", lambda: nc.gpsimd.tensor_tensor(out=rstat[:, :, 3:4], in0=rstat[:, :, 2:3], in1=mhalf[:].unsqueeze(1).broadcast_to([128, 16, 1]), op=ALU.pow), reads=[rstatb, mhalfb], writes=[rstatb])
            for c in range(16):
                A("dve", (lambda c=c: nc.vector.tensor_scalar(out=oraw[:, c * 128:(c + 1) * 128], in0=oraw[:, c * 128:(c + 1) * 128], scalar1=rstat[:, c, 0:1], scalar2=rstat[:, c, 3:4], op0=ALU.subtract, op1=ALU.mult)),
                  reads=orawb + [rstatb], writes=orawb)
            A("pool", lambda: nc.gpsimd.tensor_tensor(out=ytok, in0=oraw, in1=sgt, op=ALU.mult), reads=orawb + sgtb, writes=ytokb)
            for c4 in range(4):
                b = c4 % 2
                for cc in range(4):
                    c = c4 * 4 + cc
                    A("pe", (lambda c=c, cc=cc, b=b: nc.tensor.transpose(psb[b][:, cc * 128:(cc + 1) * 128], ytok[:, c * 128:(c + 1) * 128], id_bf)), reads=ytokb + [cbfb], writes=[pb[b]])
                evac_copy(ya[:, h, c4 * 512:(c4 + 1) * 512], psb[b][:, 0:512], [pb[b]], [yab[h]], eng="act")

        def ffn(l, s, last):
            off = _offsets(l)
            CU = [at(32 + 2 * i, 2) for i in range(2)]; CG = [at(36 + 2 * i, 2) for i in range(2)]
            SG = [at(40 + 2 * i, 2) for i in range(2)]
            aT = ya_flat[:, 0:NP * 512].rearrange("p (k t) -> p k t", t=512)
            wdv = wbig[:, 0:22528].rearrange("p (k c) -> p k c", k=NP)
            load_ln(l, 1)
            it = 0
            for tb in range(4):
                for p in range(NP):
                    wt, wtb = load_w(l, off["up"](p), 2048, 8)
                    if tb == 0 and p == 2:
                        A("sp", lambda: nc.sync.dma_start(out=wbig[:, 0:22528], in_=wb[l][:, off["down"]:off["down"] + 22528]), reads=wb_bufs(l, off["down"], 22528), writes=[wbigb], dma=True)
                    bu = (it % 2) * 2; bg = bu + 1; it += 1
                    proj_fm(wt, wtb, 0, bu, tb); proj_fm(wt, wtb, 128, bg, tb)
                    res = []
                    for (bk, ug, CC) in ((bu, 0, CU), (bg, 1, CG)):
                        c_, cb_ = CC[it % 2]
                        cw = lambda j, ug=ug, p=p: convp[:, ((l * NP + p) * 2 + ug) * 4 + j:((l * NP + p) * 2 + ug) * 4 + j + 1]
                        A("act", (lambda bk=bk, c_=c_, cw=cw: nc.scalar.activation(out=c_, in_=ps[:, bk, :], func=AF.Identity, scale=cw(2), bias=cw(3))), reads=[pb[bk], convpb], writes=cb_)
                        A("dve", (lambda bk=bk, c_=c_, cw=cw: nc.vector.scalar_tensor_tensor(out=c_[:, 1:512], in0=ps[:, bk, 0:511], scalar=cw(1), in1=c_[:, 1:512], op0=ALU.mult, op1=ALU.add)), reads=[pb[bk], convpb] + cb_, writes=cb_)
                        A("dve", (lambda bk=bk, c_=c_, cw=cw: nc.vector.scalar_tensor_tensor(out=c_[:, 2:512], in0=ps[:, bk, 0:510], scalar=cw(0), in1=c_[:, 2:512], op0=ALU.mult, op1=ALU.add)), reads=[pb[bk], convpb] + cb_, writes=cb_)
                        if tb > 0:
                            A("dve", (lambda c_=c_, cw=cw, p=p, ug=ug: nc.vector.scalar_tensor_tensor(out=c_[:, 0:2], in0=halo[:, p, ug, :], scalar=cw(0), in1=c_[:, 0:2], op0=ALU.mult, op1=ALU.add)), reads=[halob[p], convpb] + cb_, writes=cb_)
                            A("dve", (lambda c_=c_, cw=cw, p=p, ug=ug: nc.vector.scalar_tensor_tensor(out=c_[:, 0:1], in0=halo[:, p, ug, 1:2], scalar=cw(1), in1=c_[:, 0:1], op0=ALU.mult, op1=ALU.add)), reads=[halob[p], convpb] + cb_, writes=cb_)
                        if tb < 3:
                            A("dve", (lambda bk=bk, p=p, ug=ug: nc.vector.tensor_copy(out=halo[:, p, ug, :], in_=ps[:, bk, 510:512])), reads=[pb[bk]], writes=[halob[p]])
                        res.append((c_, cb_))
                    (cu, cub), (cg, cgb) = res
                    sg, sgb = SG[it % 2]
                    A("act", (lambda cg=cg, sg=sg: nc.scalar.activation(out=sg, in_=cg, func=AF.Silu)), reads=cgb, writes=sgb)
                    A("pool", (lambda cu=cu, sg=sg, p=p: nc.gpsimd.tensor_tensor(out=aT[:, p, :], in0=cu, in1=sg, op=ALU.mult)), reads=cub + sgb, writes=[yab[p // 4]])
                tiles = [tb * 4 + i for i in range(4)]
                ln_phase(tiles, (lambda k, t, tb=tb: aT[:, k, (t - tb * 4) * 128:(t - tb * 4 + 1) * 128]), yab[0:6], NP, wdv,
                         (lambda t: (xres[s, t * 128:(t + 1) * 128, :], [xresb[s][t]])),
                         (lambda t: ((out_d if last else xres)[s, t * 128:(t + 1) * 128, :], [(outb if last else xresb)[s][t]])), not last)

        xresb = [[Buf("xres%d_%d" % (s, t)) for t in range(NT)] for s in range(2)]
        outb = [[Buf("out%d_%d" % (s, t)) for t in range(NT)] for s in range(2)]
        uidx = 0
        for s in range(nseq if dbg >= 1 else 0):
            for t in range(NT):
                xn, xnb = ln_tiles[("xn", t % 2)]
                A("sp", (lambda t=t, xn=xn, s=s: nc.sync.dma_start(out=xn, in_=x_d[s, t * 128:(t + 1) * 128, :])), writes=xnb, dma=True)
                for k in range(8):
                    A("pe", (lambda k=k, xn=xn: nc.tensor.transpose(ps[:, 6 + k // 4, (k % 4) * 128:(k % 4 + 1) * 128], xn[:, k * 128:(k + 1) * 128], id_f)), reads=xnb + [cstb], writes=[pb[6 + k // 4]])
                ptr = ps[:, 6:8, :].rearrange("p a (b c) -> p (a b) c", c=128)
                A("act", (lambda t=t, ptr=ptr: nc.scalar.activation(out=xT[:, :, t * 128:(t + 1) * 128], in_=ptr, func=AF.Copy)), reads=[pb[6], pb[7]], writes=[xTb[t]])
            first = True
            for li, l in enumerate(layers):
                off = _offsets(l)
                last = (li == len(layers) - 1)
                if l % 2 == 0:
                    units = [("ret", h) for h in range(4)] + [("sb", u) for u in range(4)]
                else:
                    units = [("diff", h) for h in range(8)]
                if dbg < 2:
                    break
                if dbg == 2:
                    units = units[:1]
                elif dbg == 3:
                    units = units[:4]
                for ui, (kind, idx) in enumerate(units):
                    if kind == "ret":
                        ret_unit(l, idx, uidx)
                    elif kind == "sb":
                        sb_unit(l, idx, uidx)
                    else:
                        diff_unit(l, idx, uidx)
                    uidx += 1
                    if ui == 1:
                        A("sp", (lambda l=l, off=off: nc.sync.dma_start(out=wbig[:, 0:8192], in_=wb[l][:, off["wout"]:off["wout"] + 8192])), reads=wb_bufs(l, off["wout"], 8192), writes=[wbigb], dma=True)
                        load_ln(l, 0)
                if dbg < 5:
                    break
                wov = wbig[:, 0:8192].rearrange("p (k c) -> p k c", k=8)
                srcT = x_d if first else xres
                ln_phase(list(range(NT)), (lambda k, t: ya[:, k, t * 128:(t + 1) * 128]), yab, 8, wov,
                         (lambda t, srcT=srcT, first=first: (srcT[s, t * 128:(t + 1) * 128, :], [] if first else [xresb[s][t]])),
                         (lambda t: (xres[s, t * 128:(t + 1) * 128, :], [xresb[s][t]])), True)
                first = False
                if dbg < 6:
                    break
                ffn(l, s, last)
        allout = [b for s in range(nseq) for b in outb[s]]
        A("sp", None, reads=[], writes=allout)
        info = S.emit()
    return nc, info


def _prep_inputs(inp):
    inp = {k: np.asarray(v) for k, v in inp.items()}
    cst, _ = _consts()
    shared = {"cst": cst, "rope": _rope(), "bt": _bias_tiles(inp["rel_bias"])}
    for l in range(DEPTH):
        shared["wl%d" % l] = _layer_weights(l, inp)
    cp = np.zeros((128, DEPTH, NP, 2, 4), np.float32)
    for l in range(DEPTH):
        for ug in range(2):
            for j in range(3):
                cp[:, l, :, ug, j] = inp["conv_w"][l, j, ug * DFF:(ug + 1) * DFF].reshape(NP, 128).T
            cp[:, l, :, ug, 3] = inp["conv_b"][l, ug * DFF:(ug + 1) * DFF].reshape(NP, 128).T
    shared["convp"] = cp.reshape(128, -1)
    ln = np.stack([np.stack([inp["ln1_g"][l], inp["ln1_b"][l], inp["ln2_g"][l], inp["ln2_b"][l]]) for l in range(DEPTH)]).reshape(16, D)
    shared["lnbc"] = np.ascontiguousarray(np.broadcast_to(ln[None], (128, 16, D)), dtype=np.float32)
    lam = np.stack([np.stack([inp["lam_q1"][i], inp["lam_k1"][i], inp["lam_q2"][i], inp["lam_k2"][i]]) for i in range(2)]).reshape(-1)
    shared["lamp"] = np.ascontiguousarray(np.broadcast_to(lam[None], (128, 512)), dtype=np.float32)
    shared["subg"] = np.ascontiguousarray(inp["subln_g"].T, dtype=np.float32)
    return inp, shared


_NC_CACHE = {}


def kernel(**inputs):
    inp, shared = _prep_inputs(inputs)
    if "nc" not in _NC_CACHE:
        _NC_CACHE["nc"] = build()[0]
    nc = _NC_CACHE["nc"]
    x = np.ascontiguousarray(inp["x"], dtype=np.float32)
    in_maps = []
    for c in range(8):
        m = dict(shared); m["x"] = np.ascontiguousarray(x[2 * c:2 * c + 2])
        in_maps.append(m)
    res = run_bass_kernel_spmd(nc, in_maps, core_ids=list(range(8)))
    return np.concatenate([r["out"] for r in res.results], axis=0).astype(np.float32)
```

```python
import math, contextlib
import numpy as np
import concourse.bass as bass
import concourse.mybir as mybir
from concourse.bass_utils import run_bass_kernel_spmd

F32 = mybir.dt.float32
BF16 = mybir.dt.bfloat16
AF = mybir.ActivationFunctionType
ALU = mybir.AluOpType

D = 1024; SEQ = 2048; NT = 16; DFF = 2816; NP = 22; DEPTH = 4
ALPHA = (2 * DEPTH) ** 0.25
EPS = 1e-5
NEG = -30000.0
FL_EVEN = 112640; FL_ODD = 100352
PCH = 2048
import os
RSTOP = int(os.environ.get('RSTOP', '99'))
VG = int(os.environ.get('VG', '2'))
DSTOP = int(os.environ.get('DSTOP', '99'))
SBSKEW = int(os.environ.get('SBSKEW', '1'))
DSKEW = int(os.environ.get('DSKEW', '1'))


class Buf:
    __slots__ = ("name", "lw", "rd", "rdd", "excl")

    def __init__(self, name="", excl=False):
        self.name = name; self.lw = None; self.rd = {}; self.rdd = []; self.excl = excl


class Op:
    __slots__ = ("eng", "fn", "deps", "dma", "signal", "sem", "val", "guard")

    def __init__(self, eng, fn, deps, dma):
        self.eng = eng; self.fn = fn; self.deps = deps; self.dma = dma
        self.signal = False; self.sem = None; self.val = 0; self.guard = None


class Sched:
    def __init__(self, nc, es, n_dma_sems=32, same_engine_sync=True):
        self.nc = nc; self.ops = []; self.same = same_engine_sync
        self.h = {"pe": nc.tensor, "act": nc.scalar, "dve": nc.vector, "pool": nc.gpsimd, "sp": nc.sync}
        self.esem = {e: es.enter_context(nc.semaphore("sem_" + e)) for e in ["pe", "act", "dve", "pool"]}
        self.dsems = [es.enter_context(nc.semaphore("dsem%d" % i)) for i in range(n_dma_sems)]

    def add(self, eng, fn, reads=(), writes=(), dma=False):
        i = len(self.ops)
        deps = set()
        for b in reads:
            if b.lw is not None:
                deps.add(b.lw)
            if b.excl:
                for e2, o2 in b.rd.items():
                    if e2 != eng:
                        deps.add(o2)
        for b in writes:
            if b.lw is not None:
                deps.add(b.lw)
            deps.update(b.rd.values()); deps.update(b.rdd)
        deps.discard(i)
        for b in reads:
            if dma:
                b.rdd.append(i)
            else:
                b.rd[eng] = i
        for b in writes:
            b.lw = i; b.rd = {}; b.rdd = []
        self.ops.append(Op(eng, fn, deps, dma))
        return i

    def _skip(self, dop, op):
        if dop.fn is None:
            assert dop.eng == op.eng, "cross-engine dep on barrier"
            return True
        if (not dop.dma) and (not op.dma) and dop.eng == op.eng:
            if dop.eng == "pe" or not self.same:
                return True
        return False

    def emit(self):
        ops = self.ops
        for op in ops:
            for d in op.deps:
                if not self._skip(ops[d], op):
                    ops[d].signal = True
        for op in ops:
            if op.dma and op.fn is not None:
                op.signal = True
        cnt = {e: 0 for e in self.esem}
        dval = [0] * len(self.dsems); rr = 0
        for op in ops:
            if not op.signal:
                continue
            if op.dma:
                op.sem = self.dsems[rr]; op.guard = (self.dsems[rr], dval[rr])
                dval[rr] += 16; op.val = dval[rr]; rr = (rr + 1) % len(self.dsems)
            else:
                cnt[op.eng] += 1; op.sem = self.esem[op.eng]; op.val = cnt[op.eng]
        waited = {e: {} for e in self.h}
        nw = 0
        for op in ops:
            e = self.h[op.eng]
            need = {}
            for d in op.deps:
                dop = ops[d]
                if self._skip(dop, op):
                    continue
                k = id(dop.sem)
                if k not in need or need[k][1] < dop.val:
                    need[k] = (dop.sem, dop.val)
            if op.dma and op.signal and op.guard[1] > 0:
                k = id(op.guard[0])
                if k not in need or need[k][1] < op.guard[1]:
                    need[k] = op.guard
            w = waited[op.eng]
            for k, (sem, val) in need.items():
                if w.get(k, 0) < val:
                    e.wait_ge(sem, val); w[k] = val; nw += 1
            if op.fn is None:
                continue
            inst = op.fn()
            if op.signal:
                inst.then_inc(op.sem, 16 if op.dma else 1)
        return dict(n_ops=len(ops), n_waits=nw, cnt=cnt, dval=max(dval))


def _tile_k(w):
    K = w.shape[0] // 128
    return np.ascontiguousarray(w.reshape(K, 128, -1).transpose(1, 0, 2)).reshape(128, -1)


def _layer_weights(l, inp):
    parts = []
    i = l // 2
    if l % 2 == 0:
        win = inp["w_in_even"][i]; wout = inp["w_out_even"][i]
        for h in range(4):
            q = win[:, h * 128:(h + 1) * 128]; qs = np.concatenate([q[:, 64:], q[:, :64]], 1)
            k = win[:, 512 + h * 128:512 + (h + 1) * 128]; ks = np.concatenate([k[:, 64:], k[:, :64]], 1)
            v = win[:, 1024 + h * 128:1024 + (h + 1) * 128]; g = win[:, 1536 + h * 128:1536 + (h + 1) * 128]
            for pair in ([q, qs], [k, ks], [v, g]):
                parts.append(_tile_k(np.concatenate(pair, 1)))
        for u in range(4):
            q = win[:, 2048 + u * 128:2048 + (u + 1) * 128]; k = win[:, 2560 + u * 128:2560 + (u + 1) * 128]
            v = win[:, 3072 + u * 128:3072 + (u + 1) * 128]
            parts.append(_tile_k(np.concatenate([q, k], 1))); parts.append(_tile_k(v))
    else:
        win = inp["w_in_odd"][i]; wout = inp["w_out_odd"][i]
        for h in range(8):
            q = win[:, h * 128:(h + 1) * 128]; k = win[:, 1024 + h * 128:1024 + (h + 1) * 128]
            v = win[:, 2048 + h * 128:2048 + (h + 1) * 128]
            parts.append(_tile_k(np.concatenate([q, k], 1))); parts.append(_tile_k(v))
    parts.append(_tile_k(wout))
    wup = inp["w_up"][l]
    for p in range(NP):
        parts.append(_tile_k(np.concatenate([wup[:, p * 128:(p + 1) * 128], wup[:, DFF + p * 128:DFF + (p + 1) * 128]], 1)))
    parts.append(_tile_k(inp["w_down"][l]))
    out = np.ascontiguousarray(np.concatenate(parts, 1), dtype=np.float32)
    assert out.shape[1] == (FL_EVEN if l % 2 == 0 else FL_ODD)
    return out


def _offsets(l):
    if l % 2 == 0:
        return dict(ret=lambda h, j: h * 6144 + j * 2048, qk=lambda u: 24576 + u * 3072, v=lambda u: 24576 + u * 3072 + 2048,
                    wout=36864, up=lambda p: 45056 + p * 2048, down=90112)
    return dict(qk=lambda u: u * 3072, v=lambda u: u * 3072 + 2048, wout=24576, up=lambda p: 32768 + p * 2048, down=77824)


CST_DEC = 0; CST_QDEC = 512; CST_KDEC = 1024; CST_SBM = 1028; CST_U = 1156; CST_ID = 1284; CST_W = 1412


def _consts():
    c = np.zeros((128, CST_W), np.float32)
    lg = np.log(1.0 - 2.0 ** (-5.0 - np.arange(4, dtype=np.float64)))
    idx = np.arange(128, dtype=np.float64)
    for h in range(4):
        rel = idx[None, :] - idx[:, None]
        dec = np.where(rel >= 0, np.exp(lg[h] * np.maximum(rel, 0.0)), 0.0) / math.sqrt(128.0)
        c[:, CST_DEC + h * 128:CST_DEC + (h + 1) * 128] = dec
        c[:, CST_QDEC + h * 128:CST_QDEC + (h + 1) * 128] = np.exp(lg[h] * (idx + 1.0))[None, :]
        c[:, CST_KDEC + h] = np.exp(lg[h] * (127.0 - idx)) / math.sqrt(128.0)
    s_i = idx[:, None]; i_i = idx[None, :]
    c[:, CST_SBM:CST_SBM + 128] = np.where(s_i < i_i, 0.0, NEG)
    c[:, CST_U:CST_U + 128] = (idx[:, None] > idx[None, :]).astype(np.float32)
    c[:, CST_ID:CST_ID + 128] = np.eye(128)
    chunk_g = [float(np.exp(lg[h] * 128.0)) for h in range(4)]
    return c, chunk_g


def _rope():
    inv = (10000.0 ** (-np.arange(0, 128, 2, dtype=np.float32) / np.float32(128))).astype(np.float32)
    ang = np.arange(SEQ, dtype=np.float32)[:, None] * inv[None, :]
    cos = np.cos(ang).T.astype(np.float32); sin = np.sin(ang).T.astype(np.float32)
    r = np.zeros((2, 128, SEQ), np.float32)
    r[0, :64] = cos; r[0, 64:] = cos
    r[1, :64] = -sin; r[1, 64:] = sin
    return r


def _t5_bucket(rel):
    n = np.maximum(rel, 0)
    nf = np.maximum(n, 1).astype(np.float32)
    large = 16 + (np.log(nf / np.float32(16)) / np.float32(math.log(128 / 16)) * np.float32(16)).astype(np.int32)
    large = np.minimum(large, 31)
    return np.where(n < 16, n, large)


def _bias_tiles(rel_bias):
    s = np.arange(128)[:, None]; i = np.arange(128)[None, :]
    b0 = _t5_bucket(i - s); b1 = _t5_bucket(128 + i - s)
    bt = np.zeros((128, 8, 2, 128), np.float32)
    for h in range(8):
        bt[:, h, 0, :] = np.where(i >= s, rel_bias[b0, h], NEG)
        bt[:, h, 1, :] = rel_bias[b1, h]
    c31 = np.broadcast_to(rel_bias[31][None, :], (128, 8))
    return np.ascontiguousarray(np.concatenate([bt.reshape(128, -1), c31], 1), dtype=np.float32)


def build(layers=(0, 1, 2, 3), nseq=2, same_sync=True, dbg=99):
    nc = bass.Bass("TRN2", target_bir_lowering=False, dynamic_dma_scratch_size=256)
    cst_np, chunk_g = _consts()
    x_d = nc.dram_tensor("x", [2, SEQ, D], F32, kind="ExternalInput").ap()
    out_d = nc.dram_tensor("out", [2, SEQ, D], F32, kind="ExternalOutput").ap()
    xres = nc.dram_tensor("xres", [2, SEQ, D], F32, kind="Internal").ap()
    wl = {}; wb = {}
    for l in layers:
        FL = FL_EVEN if l % 2 == 0 else FL_ODD
        wl[l] = nc.dram_tensor("wl%d" % l, [128, FL], F32, kind="ExternalInput").ap()
        wb[l] = nc.dram_tensor("wb%d" % l, [128, FL], BF16, kind="Internal").ap()
    cst_d = nc.dram_tensor("cst", [128, CST_W], F32, kind="ExternalInput").ap()
    rope_d = nc.dram_tensor("rope", [2, 128, SEQ], F32, kind="ExternalInput").ap()
    bt_d = nc.dram_tensor("bt", [128, 2056], F32, kind="ExternalInput").ap()
    convp_d = nc.dram_tensor("convp", [128, DEPTH * NP * 8], F32, kind="ExternalInput").ap()
    lnbc_d = nc.dram_tensor("lnbc", [128, 16, D], F32, kind="ExternalInput").ap()
    lamp_d = nc.dram_tensor("lamp", [128, 2 * 4 * 64], F32, kind="ExternalInput").ap()
    subg_d = nc.dram_tensor("subg", [128, 2], F32, kind="ExternalInput").ap()

    es = contextlib.ExitStack()
    with es:
        S = Sched(nc, es, same_engine_sync=same_sync)
        A = S.add

        def sb(n, shp, dt=F32):
            return es.enter_context(nc.sbuf_tensor("s_" + n, shp, dt))

        ps = es.enter_context(nc.psum_tensor("ps", [128, 8, 512], F32))
        pb = [Buf("pb%d" % i, excl=True) for i in range(8)]
        psb = [ps[:, i, :].bitcast(BF16) for i in range(8)]

        xT = sb("xT", [128, 8, SEQ], BF16); xTb = [Buf("xT%d" % t) for t in range(NT)]
        ya = sb("ya", [128, 8, SEQ], BF16); yab = [Buf("ya%d" % c) for c in range(8)]
        ya_flat = ya[:].rearrange("p k t -> p (k t)")
        wbig = sb("wbig", [128, 22528], BF16); wbigb = Buf("wbig")
        NSLOT = 4
        wsl = [sb("wsl%d" % i, [128, 2048], BF16) for i in range(NSLOT)]; wslb = [Buf("wsl%d" % i) for i in range(NSLOT)]
        slot_rr = [0]
        cst = sb("cst", [128, CST_W]); cstb = Buf("cst")
        cbf = sb("cbf", [128, 384], BF16); cbfb = Buf("cbf")
        U_bf = cbf[:, 0:128]; ones_bf = cbf[:, 128:256]; id_bf = cbf[:, 256:384]
        id_f = cst[:, CST_ID:CST_ID + 128]
        btt = sb("btt", [128, 2056]); bttb = Buf("btt")
        convp = sb("convp", [128, DEPTH * NP * 8]); convpb = Buf("convp")
        lnt = sb("lnt", [128, 2, D]); lntb = Buf("lnt")
        lamp = sb("lamp", [128, 512]); lampb = Buf("lamp")
        subg = sb("subg", [128, 2]); subgb = Buf("subg")
        lamt = sb("lamt", [128, 8]); lamtb = Buf("lamt")
        mhalf = sb("mhalf", [128, 1]); mhalfb = Buf("mhalf")
        vsb = [sb("vsb%d" % i, [128, 16, 130], BF16) for i in range(2)]; vsbb = [Buf("vsb%d" % i) for i in range(2)]
        halo = sb("halo", [128, NP, 2, 2]); halob = [Buf("halo%d" % p) for p in range(NP)]
        fx = sb("fx", [128, NP, 2, 2]); fxb = Buf("fx")
        fxt = sb("fxt", [128, NP, 2, 1]); fxtb = Buf("fxt")
        NST = 8
        stt = [sb("stt%d" % i, [128, 16]) for i in range(NST)]; sttb = [Buf("stt%d" % i) for i in range(NST)]
        st_rr = [0]
        rstat = sb("rstat", [128, 16, 8]); rstatb = Buf("rstat")
        bnst = sb("bnst", [128, 16, 6]); bnstb = Buf("bnst")
        bnd = sb("bnd", [128, 8]); bndb = Buf("bnd")
        NAR = 60
        arena = sb("arena", [128, NAR * 256]); arb = [Buf("ar%d" % i) for i in range(NAR)]

        def at(u0, nun, dt=F32):
            a = arena[:, u0 * 256:(u0 + nun) * 256]
            if dt == BF16:
                a = a.bitcast(BF16)
            return a, arb[u0:u0 + nun]

        def stat():
            i = st_rr[0]; st_rr[0] = (i + 1) % NST
            return stt[i], sttb[i]

        def wb_bufs(l, off, ln):
            return wbb[l][off // PCH:(off + ln - 1) // PCH + 1]

        def load_w(l, off, ln, K):
            i = slot_rr[0]; slot_rr[0] = (i + 1) % NSLOT
            dst = wsl[i][:, 0:ln]
            A("sp", lambda: nc.sync.dma_start(out=dst, in_=wb[l][:, off:off + ln]), reads=wb_bufs(l, off, ln), writes=[wslb[i]], dma=True)
            return wsl[i][:, 0:ln].rearrange("p (k c) -> p k c", k=K), wslb[i]

        A("sp", lambda: nc.sync.dma_start(out=cst[:], in_=cst_d[:, :]), writes=[cstb], dma=True)
        A("sp", lambda: nc.sync.dma_start(out=btt[:], in_=bt_d[:, :]), writes=[bttb], dma=True)
        A("sp", lambda: nc.sync.dma_start(out=convp[:], in_=convp_d[:, :]), writes=[convpb], dma=True)
        A("sp", lambda: nc.sync.dma_start(out=lamp[:], in_=lamp_d[:, :]), writes=[lampb], dma=True)
        A("sp", lambda: nc.sync.dma_start(out=subg[:], in_=subg_d[:, :]), writes=[subgb], dma=True)
        A("dve", lambda: nc.vector.tensor_copy(out=cbf[:, 0:128], in_=cst[:, CST_U:CST_U + 128]), reads=[cstb], writes=[cbfb])
        A("dve", lambda: nc.vector.memset(cbf[:, 128:256], 1.0), writes=[cbfb])
        A("dve", lambda: nc.vector.tensor_copy(out=cbf[:, 256:384], in_=cst[:, CST_ID:CST_ID + 128]), reads=[cstb], writes=[cbfb])
        A("dve", lambda: nc.vector.memset(mhalf[:], -0.5), writes=[mhalfb])
        for i in range(2):
            A("dve", (lambda i=i: nc.vector.memset(vsb[i][:, :, 128:130], 1.0)), writes=[vsbb[i]])
        for l in layers:
            if l % 2 == 1:
                i = l // 2
                lam_init = 0.8 - 0.6 * math.exp(-0.3 * l)
                t, tb_ = stat()
                pr = sb("lamprod%d" % i, [128, 128])
                prb = Buf()
                base = i * 256
                A("dve", (lambda base=base, pr=pr: nc.vector.tensor_tensor(out=pr[:, 0:64], in0=lamp[:, base:base + 64], in1=lamp[:, base + 64:base + 128], op=ALU.mult)), reads=[lampb], writes=[prb])
                A("dve", (lambda base=base, pr=pr: nc.vector.tensor_tensor(out=pr[:, 64:128], in0=lamp[:, base + 128:base + 192], in1=lamp[:, base + 192:base + 256], op=ALU.mult)), reads=[lampb, prb], writes=[prb])
                A("dve", (lambda pr=pr, t=t: nc.vector.reduce_sum(out=t[:, 0:2], in_=pr[:].rearrange("p (a b) -> p a b", a=2), axis=mybir.AxisListType.X)), reads=[prb], writes=[tb_])
                A("act", (lambda t=t: nc.scalar.activation(out=t[:, 2:4], in_=t[:, 0:2], func=AF.Exp)), reads=[tb_], writes=[tb_])
                A("dve", (lambda t=t, i=i, li=lam_init: nc.vector.scalar_tensor_tensor(out=lamt[:, i:i + 1], in0=t[:, 3:4], scalar=-li, in1=t[:, 2:3], op0=ALU.add, op1=ALU.subtract)), reads=[tb_], writes=[lamtb])

        wbb = {}
        st32 = [wbig[:, i * 4096:(i + 1) * 4096].bitcast(F32) for i in range(3)]
        st16 = [wbig[:, 12288 + i * 2048:12288 + (i + 1) * 2048] for i in range(3)]
        st32b = [Buf() for _ in range(3)]; st16b = [Buf() for _ in range(3)]
        chunks = []
        for l in layers:
            FL = FL_EVEN if l % 2 == 0 else FL_ODD
            wbb[l] = [Buf("wb%d_%d" % (l, c)) for c in range(FL // PCH)]
            chunks += [(l, c) for c in range(FL // PCH)]

        def prep_load(idx):
            l, c = chunks[idx]; j = idx % 3
            A("sp", (lambda: nc.sync.dma_start(out=st32[j], in_=wl[l][:, c * PCH:(c + 1) * PCH])), writes=[st32b[j]], dma=True)

        def prep_cast_store(idx):
            l, c = chunks[idx]; j = idx % 3
            if idx % 2 == 0:
                A("dve", (lambda: nc.vector.tensor_copy(out=st16[j], in_=st32[j])), reads=[st32b[j]], writes=[st16b[j]])
            else:
                A("pool", (lambda: nc.gpsimd.tensor_copy(out=st16[j], in_=st32[j])), reads=[st32b[j]], writes=[st16b[j]])
            A("sp", (lambda: nc.sync.dma_start(out=wb[l][:, c * PCH:(c + 1) * PCH], in_=st16[j])), reads=[st16b[j]], writes=[wbb[l][c]], dma=True)

        for idx in range(min(2, len(chunks))):
            prep_load(idx)
        for idx in range(len(chunks)):
            if idx + 2 < len(chunks):
                prep_load(idx + 2)
            prep_cast_store(idx)
        A("sp", None, reads=[], writes=st32b + st16b + [wbigb])

        evac_rr = [0]

        def evac_copy(out, in_, reads, writes, eng=None):
            if eng is None:
                eng = "act" if evac_rr[0] % 2 == 0 else "dve"; evac_rr[0] += 1
            if eng == "act":
                A("act", lambda: nc.scalar.activation(out=out, in_=in_, func=AF.Copy), reads=reads, writes=writes)
            else:
                A("dve", lambda: nc.vector.tensor_copy(out=out, in_=in_), reads=reads, writes=writes)

        def proj_fm(wt, wtb, col0, bank, tb):
            for k in range(8):
                A("pe", (lambda k=k: nc.tensor.matmul(ps[:, bank, :], lhsT=wt[:, k, col0:col0 + 128], rhs=xT[:, k, tb * 512:(tb + 1) * 512], start=(k == 0), stop=(k == 7))),
                  reads=[wtb] + xTb[tb * 4:tb * 4 + 4], writes=[pb[bank]])

        def proj_tm(wt, wtb, col0, ncol, bank, boff, t):
            for k in range(8):
                A("pe", (lambda k=k: nc.tensor.matmul(ps[:, bank, boff:boff + ncol], lhsT=xT[:, k, t * 128:(t + 1) * 128], rhs=wt[:, k, col0:col0 + ncol], start=(k == 0), stop=(k == 7))),
                  reads=[wtb, xTb[t]], writes=[pb[bank]])

        ln_tiles = {}
        for nm, u0 in (("xt", 0), ("z", 8), ("zn", 16), ("xn", 24)):
            for j in range(2):
                ln_tiles[(nm, j)] = at(u0 + 4 * j, 4)

        def ln_stage_a(t, lhs_fn, lhs_bufs, K, wv, src, srcb):
            j = t % 2
            xt, xtb = ln_tiles[("xt", j)]; z, zb = ln_tiles[("z", j)]
            for half in range(2):
                for k in range(K):
                    A("pe", (lambda half=half, k=k: nc.tensor.matmul(ps[:, 4 + half, :], lhsT=lhs_fn(k), rhs=wv[:, k, half * 512:(half + 1) * 512], start=(k == 0), stop=(k == K - 1))),
                      reads=lhs_bufs + [wbigb], writes=[pb[4 + half]])
            A("sp", lambda: nc.sync.dma_start(out=xt, in_=src), reads=srcb, writes=xtb, dma=True)
            pz = ps[:, 4:6, :].rearrange("p a b -> p (a b)")
            A("dve", lambda: nc.vector.scalar_tensor_tensor(out=z, in0=xt, scalar=ALPHA, in1=pz, op0=ALU.mult, op1=ALU.add), reads=xtb + [pb[4], pb[5]], writes=zb)
            st, stb = stat()
            for half in range(2):
                A("dve", (lambda half=half: nc.vector.bn_stats(out=bnst[:, half, :], in_=z[:, half * 512:(half + 1) * 512])), reads=zb, writes=[bnstb])
            A("dve", lambda: nc.vector.bn_aggr(out=st[:, 0:2], in_=bnst[:, 0:2, :].rearrange("p a b -> p (a b)")), reads=[bnstb], writes=[stb])
            A("dve", lambda: nc.vector.tensor_scalar(out=st[:, 2:3], in0=st[:, 1:2], scalar1=EPS, scalar2=None, op0=ALU.add), reads=[stb], writes=[stb])
            A("pool", lambda: nc.gpsimd.tensor_tensor(out=st[:, 3:4], in0=st[:, 2:3], in1=mhalf[:], op=ALU.pow), reads=[stb, mhalfb], writes=[stb])
            A("dve", lambda: nc.vector.scalar_tensor_tensor(out=st[:, 4:5], in0=st[:, 0:1], scalar=-1.0, in1=st[:, 3:4], op0=ALU.mult, op1=ALU.mult), reads=[stb], writes=[stb])
            return (t, j, st, stb)

        def ln_stage_b(state, dst, dstb, do_transpose):
            t, j, st, stb = state
            z, zb = ln_tiles[("z", j)]; zn, znb = ln_tiles[("zn", j)]; xn, xnb = ln_tiles[("xn", j)]
            A("act", lambda: nc.scalar.activation(out=zn, in_=z, func=AF.Identity, scale=st[:, 3:4], bias=st[:, 4:5]), reads=zb + [stb], writes=znb)
            A("dve", lambda: nc.vector.tensor_tensor(out=zn, in0=zn, in1=lnt[:, 0, :], op=ALU.mult), reads=znb + [lntb], writes=znb)
            A("pool", lambda: nc.gpsimd.tensor_tensor(out=xn, in0=zn, in1=lnt[:, 1, :], op=ALU.add), reads=znb + [lntb], writes=xnb)
            A("sp", lambda: nc.sync.dma_start(out=dst, in_=xn), reads=xnb, writes=dstb, dma=True)
            if do_transpose:
                for k in range(8):
                    A("pe", (lambda k=k: nc.tensor.transpose(ps[:, 6 + k // 4, (k % 4) * 128:(k % 4 + 1) * 128], xn[:, k * 128:(k + 1) * 128], id_f)),
                      reads=xnb + [cstb], writes=[pb[6 + k // 4]])
                ptr = ps[:, 6:8, :].rearrange("p a (b c) -> p (a b) c", c=128)
                A("act", lambda: nc.scalar.activation(out=xT[:, :, t * 128:(t + 1) * 128], in_=ptr, func=AF.Copy), reads=[pb[6], pb[7]], writes=[xTb[t]])

        def ln_phase(tiles, lhs_fn_t, lhs_bufs, K, wv, src_fn, dst_fn, do_transpose):
            pend = None
            for t in tiles:
                src, srcb = src_fn(t)
                stt_ = ln_stage_a(t, (lambda k, t=t: lhs_fn_t(k, t)), lhs_bufs, K, wv, src, srcb)
                if pend is not None:
                    d, db = dst_fn(pend[0]); ln_stage_b(pend, d, db, do_transpose)
                pend = stt_
            d, db = dst_fn(pend[0]); ln_stage_b(pend, d, db, do_transpose)

        def load_ln(l, which):
            A("sp", lambda: nc.sync.dma_start(out=lnt[:], in_=lnbc_d[:, l * 4 + which * 2:l * 4 + which * 2 + 2, :]), writes=[lntb], dma=True)

        def attn_proj(l, u, uidx):
            off = _offsets(l)
            base = (uidx % 2) * 12
            qz = [at(base, 4, BF16), at(base + 4, 4, BF16)]
            kT, kTb = at(base + 8, 4, BF16)
            wqk, wqkb = load_w(l, off["qk"](u), 2048, 8)
            wv_, wvb = load_w(l, off["v"](u), 1024, 8)
            A("pool", lambda: nc.gpsimd.memset(qz[0][0][64:128, :], 0.0), writes=qz[0][1])
            A("pool", lambda: nc.gpsimd.memset(qz[1][0][0:64, :], 0.0), writes=qz[1][1])
            bank = 0
            for tb in range(4):
                b = bank % 3; bank += 1
                proj_fm(wqk, wqkb, 0, b, tb)
                A("act", (lambda tb=tb, b=b: nc.scalar.activation(out=qz[0][0][0:64, tb * 512:(tb + 1) * 512], in_=ps[0:64, b, :], func=AF.Copy)), reads=[pb[b]], writes=qz[0][1])
                A("dve", (lambda tb=tb, b=b: nc.vector.tensor_copy(out=qz[1][0][64:128, tb * 512:(tb + 1) * 512], in_=ps[64:128, b, :])), reads=[pb[b]], writes=qz[1][1])
            for tb in range(4):
                b = bank % 3; bank += 1
                proj_fm(wqk, wqkb, 128, b, tb)
                evac_copy(kT[:, tb * 512:(tb + 1) * 512], ps[:, b, :], [pb[b]], kTb)
            vi = uidx % 2
            for t4 in range(4):
                b = bank % 3; bank += 1
                for tt in range(4):
                    proj_tm(wv_, wvb, 0, 128, b, tt * 128, t4 * 4 + tt)
                evac_copy(vsb[vi][:, t4 * 4:t4 * 4 + 4, 0:128], ps[:, b, :].rearrange("p (a c) -> p a c", c=128), [pb[b]], [vsbb[vi]])
            return qz, kT, kTb, vsb[vi], vsbb[vi]

        def diff_unit(l, h, uidx):
            i_odd = l // 2
            lam_init = 0.8 - 0.6 * math.exp(-0.3 * l)
            csc = (1.0 - lam_init) ** -2
            qz, kT, kTb, v, vb = attn_proj(l, h, uidx)
            if DSTOP < 2: return
            PT = [at(24 + i, 1, BF16) for i in range(3)]
            TMP = [at(27 + i, 1) for i in range(4)]
            o0, o0b = at(31, 2)
            OW = [at(33 + i, 1) for i in range(4)]
            ONB = [at(37 + i, 1, BF16) for i in range(2)]
            cnt = dict(tmp=0, ow=0, on=0)
            steps = [(g, m, kb) for g in range(4) for m in range(2) for kb in range(0, 4 * g + 4)]
            n = len(steps)

            def geom(i):
                g, m, kb = steps[i]
                r = kb - 4 * g; c0 = max(r, 0) * 128
                return g, m, kb, r, c0, 512 - c0, i % 3

            def st_S(i):
                g, m, kb, r, c0, N, b = geom(i)
                q_, qb_ = qz[m]
                A("pe", lambda: nc.tensor.matmul(ps[:, b, 0:N], lhsT=kT[:, kb * 128:(kb + 1) * 128], rhs=q_[:, g * 512 + c0:(g + 1) * 512], start=True, stop=True),
                  reads=kTb + qb_, writes=[pb[b]])

            def st_P(i):
                g, m, kb, r, c0, N, b = geom(i)
                pt, ptb = PT[i % 3]
                j0 = max(r, 0)
                far0 = None
                for j in range(j0, 4):
                    d = 4 * g + j - kb
                    lc = j * 128 - c0
                    if d <= 1:
                        tm, tmb = TMP[cnt["tmp"] % 4]; cnt["tmp"] += 1
                        tmv = tm[:, 0:128]
                        bto = h * 256 + d * 128
                        A("dve", (lambda lc=lc, tmv=tmv, bto=bto: nc.vector.scalar_tensor_tensor(out=tmv, in0=ps[:, b, lc:lc + 128], scalar=0.125, in1=btt[:, bto:bto + 128], op0=ALU.mult, op1=ALU.add)),
                          reads=[pb[b], bttb], writes=tmb)
                        A("act", (lambda lc=lc, tmv=tmv: nc.scalar.activation(out=pt[:, lc:lc + 128], in_=tmv, func=AF.Exp)), reads=tmb, writes=ptb)
                    elif far0 is None:
                        far0 = lc
                if far0 is not None:
                    A("act", (lambda far0=far0: nc.scalar.activation(out=pt[:, far0:N], in_=ps[:, b, far0:N], func=AF.Exp, scale=0.125, bias=btt[:, 2048 + h:2049 + h])),
                      reads=[pb[b], bttb], writes=ptb)
                for j in range(j0, 4):
                    lc = j * 128 - c0
                    A("pe", (lambda j=j, lc=lc: nc.tensor.matmul(ps[:, 3 + j, 0:129], lhsT=pt[:, lc:lc + 128], rhs=v[:, kb, 0:129], start=(kb == 0), stop=(kb == 4 * g + j))),
                      reads=ptb + [vb], writes=[pb[3 + j]])
                if kb == 4 * g + 3:
                    finalize(g, m)

            def finalize(g, m):
                for j in range(4):
                    st, stb = stat()
                    A("dve", (lambda j=j, st=st: nc.vector.reciprocal(out=st[:, 0:1], in_=ps[:, 3 + j, 128:129])), reads=[pb[3 + j]], writes=[stb])
                    if m == 0:
                        A("act", (lambda j=j, st=st: nc.scalar.activation(out=o0[:, j * 128:(j + 1) * 128], in_=ps[:, 3 + j, 0:128], func=AF.Identity, scale=st[:, 0:1])), reads=[pb[3 + j], stb], writes=o0b)
                    else:
                        ow, owb = OW[cnt["ow"] % 4]; cnt["ow"] += 1
                        o1 = ow[:, 0:128]; oo = ow[:, 128:256]
                        A("act", (lambda j=j, st=st, o1=o1: nc.scalar.activation(out=o1, in_=ps[:, 3 + j, 0:128], func=AF.Identity, scale=st[:, 0:1])), reads=[pb[3 + j], stb], writes=owb)
                        A("dve", (lambda j=j, o1=o1, oo=oo: nc.vector.scalar_tensor_tensor(out=oo, in0=o1, scalar=lamt[:, i_odd:i_odd + 1], in1=o0[:, j * 128:(j + 1) * 128], op0=ALU.mult, op1=ALU.add)),
                          reads=owb + o0b + [lamtb], writes=owb)
                        A("dve", (lambda oo=oo: nc.vector.bn_stats(out=bnd[:, 0:6], in_=oo)), reads=owb, writes=[bndb])
                        A("dve", (lambda st=st: nc.vector.bn_aggr(out=st[:, 4:6], in_=bnd[:, 0:6])), reads=[bndb], writes=[stb])
                        A("dve", (lambda st=st: nc.vector.scalar_tensor_tensor(out=st[:, 1:2], in0=st[:, 4:5], scalar=st[:, 4:5], in1=st[:, 5:6], op0=ALU.mult, op1=ALU.add)), reads=[stb], writes=[stb])
                        A("dve", (lambda st=st: nc.vector.tensor_scalar(out=st[:, 2:3], in0=st[:, 1:2], scalar1=csc, scalar2=EPS * csc, op0=ALU.mult, op1=ALU.add)), reads=[stb], writes=[stb])
                        A("pool", (lambda st=st: nc.gpsimd.tensor_tensor(out=st[:, 3:4], in0=st[:, 2:3], in1=mhalf[:], op=ALU.pow)), reads=[stb, mhalfb], writes=[stb])
                        onb, onbb = ONB[cnt["on"] % 2]; cnt["on"] += 1
                        onv = onb[:, 0:128]
                        A("act", (lambda st=st, oo=oo, onv=onv: nc.scalar.activation(out=onv, in_=oo, func=AF.Identity, scale=st[:, 3:4])), reads=owb + [stb], writes=onbb)
                        A("pe", (lambda j=j, onv=onv: nc.tensor.transpose(psb[7][:, j * 128:(j + 1) * 128], onv, id_bf)), reads=onbb + [cbfb], writes=[pb[7]])
                if m == 1:
                    A("act", lambda: nc.scalar.activation(out=ya[:, h, g * 512:(g + 1) * 512], in_=psb[7][:, 0:512], func=AF.Identity, scale=subg[:, i_odd:i_odd + 1]), reads=[pb[7], subgb], writes=[yab[h]])

            for i in range(min(DSKEW, n)):
                st_S(i)
            for i in range(n):
                if i + DSKEW < n:
                    st_S(i + DSKEW)
                st_P(i)

        def sb_unit(l, u, uidx):
            qz, kT, kTb, v, vb = attn_proj(l, u, uidx)
            AT = [at(24 + i, 1, BF16) for i in range(3)]
            LB = [at(27 + i, 1, BF16) for i in range(3)]
            E = [at(30 + 2 * i, 2) for i in range(2)]
            TT = [at(34 + 2 * i, 2) for i in range(2)]
            ZD = [at(38 + i, 1) for i in range(2)]
            sbm = cst[:, CST_SBM:CST_SBM + 128]
            steps = []
            gi = 0
            for g in range(4):
                for a in range(2):
                    for kb in range(4 * g + 3, -1, -1):
                        steps.append((g, a, kb, 6 + gi % 2))
                    gi += 1
            n = len(steps)

            def geom(i):
                g, a, kb, ob = steps[i]
                r = kb - 4 * g; c0 = max(r, 0) * 128; N = 512 - c0
                nd = 128 if r >= 0 else 0
                return g, a, kb, ob, r, c0, N, nd

            def st_Z(i):
                g, a, kb, ob, r, c0, N, nd = geom(i)
                b = i % 3
                q_, qb_ = qz[a]
                A("pe", lambda: nc.tensor.matmul(ps[:, b, 0:N], lhsT=kT[:, kb * 128:(kb + 1) * 128], rhs=q_[:, g * 512 + c0:(g + 1) * 512], start=True, stop=True),
                  reads=kTb + qb_, writes=[pb[b]])

            def st_EL(i):
                g, a, kb, ob, r, c0, N, nd = geom(i)
                zb_ = i % 3; e, eb = E[i % 2]; lb, lbb = LB[i % 3]; cb = 3 + i % 2
                if r >= 0:
                    zd, zdb = ZD[i % 2]; zdv = zd[:, 0:128]
                    A("dve", lambda: nc.vector.scalar_tensor_tensor(out=zdv, in0=ps[:, zb_, 0:128], scalar=0.125, in1=sbm, op0=ALU.mult, op1=ALU.add), reads=[pb[zb_], cstb], writes=zdb)
                    A("act", lambda: nc.scalar.activation(out=e[:, 0:128], in_=zdv, func=AF.Exp), reads=zdb, writes=eb)
                if N > nd:
                    A("act", lambda: nc.scalar.activation(out=e[:, nd:N], in_=ps[:, zb_, nd:N], func=AF.Exp, scale=0.125), reads=[pb[zb_]], writes=eb)
                A("act", lambda: nc.scalar.activation(out=lb[:, 0:N], in_=e[:, 0:N], func=AF.Ln, bias=1.0), reads=eb, writes=lbb)
                A("pe", lambda: nc.tensor.matmul(ps[:, cb, 0:N], lhsT=U_bf, rhs=lb[:, 0:N], start=True, stop=True), reads=lbb + [cbfb], writes=[pb[cb]])

            def st_D(i):
                g, a, kb, ob, r, c0, N, nd = geom(i)
                zb_ = i % 3; lb, lbb = LB[i % 3]; cb = 3 + i % 2; tt, ttb = TT[i % 2]; att, attb = AT[i % 3]
                if r >= 0:
                    zd, zdb = ZD[i % 2]; zdv = zd[:, 0:128]
                    A("dve", lambda: nc.vector.tensor_tensor(out=tt[:, 0:128], in0=zdv, in1=lb[:, 0:128], op=ALU.subtract), reads=zdb + lbb, writes=ttb)
                if N > nd:
                    A("dve", lambda: nc.vector.scalar_tensor_tensor(out=tt[:, nd:N], in0=ps[:, zb_, nd:N], scalar=0.125, in1=lb[:, nd:N], op0=ALU.mult, op1=ALU.subtract),
                      reads=[pb[zb_]] + lbb, writes=ttb)
                A("dve", lambda: nc.vector.scalar_tensor_tensor(out=tt[:, 0:N], in0=ps[:, cb, 0:N], scalar=-1.0, in1=tt[:, 0:N], op0=ALU.mult, op1=ALU.add), reads=[pb[cb]] + ttb, writes=ttb)
                if N > nd and kb != 4 * g + 3:
                    A("dve", lambda: nc.vector.scalar_tensor_tensor(out=tt[:, nd:N], in0=ps[:, 5, c0 + nd:512], scalar=-1.0, in1=tt[:, nd:N], op0=ALU.mult, op1=ALU.add), reads=[pb[5]] + ttb, writes=ttb)
                if kb >= 1:
                    A("pe", lambda: nc.tensor.matmul(ps[:, 5, c0:512], lhsT=ones_bf, rhs=lb[:, 0:N], start=(kb == 4 * g + 3), stop=True, skip_group_check=True), reads=lbb + [cbfb], writes=[pb[5]])
                A("act", lambda: nc.scalar.activation(out=att[:, 0:N], in_=tt[:, 0:N], func=AF.Exp), reads=ttb, writes=attb)
                A("pe", lambda: nc.tensor.matmul(ps[:, ob, c0:512], lhsT=v[:, kb, 0:128], rhs=att[:, 0:N], start=(kb == 4 * g + 3), stop=(kb == 0), skip_group_check=True), reads=attb + [vb], writes=[pb[ob]])
                if kb == 0:
                    lo = a * 64
                    evac_copy(ya[lo:lo + 64, 4 + u, g * 512:(g + 1) * 512], ps[lo:lo + 64, ob, :], [pb[ob]], [yab[4 + u]])

            if SBSKEW:
                for i in range(-2, n):
                    if 0 <= i + 2 < n:
                        st_Z(i + 2)
                    if 0 <= i + 1 < n:
                        st_EL(i + 1)
                    if i >= 0:
                        st_D(i)
            else:
                for i in range(n):
                    st_Z(i); st_EL(i); st_D(i)

        def ret_unit(l, h, uidx):
            off = _offsets(l)
            qT, qTb = at((uidx % 2) * 8, 4, BF16); kT, kTb = at((uidx % 2) * 8 + 4, 4, BF16)
            qd, qdb = at(16, 4, BF16); kd, kdb = at(20, 4, BF16)
            oraw, orawb = at(24, 8); sgt, sgtb = at(32, 4, BF16); ytok, ytokb = at(36, 4, BF16)
            SD = [at(40 + i, 1, BF16) for i in range(2)]
            stf, stfb = at(42, 1); stbf, stbfb = at(43, 1, BF16)
            R1 = [at(44 + 2 * i, 2) for i in range(2)]; R2 = [at(48 + 2 * i, 2) for i in range(2)]
            CS = [at(52 + 4 * i, 4) for i in range(2)]
            w_qq, w_qqb = load_w(l, off["ret"](h, 0), 2048, 8)
            w_kk, w_kkb = load_w(l, off["ret"](h, 1), 2048, 8)
            w_vg, w_vgb = load_w(l, off["ret"](h, 2), 2048, 8)
            vi = uidx % 2; v = vsb[vi]; vb = vsbb[vi]
            bank = 0
            for tb in range(4):
                cs, csb = CS[tb % 2]
                csv = cs.rearrange("p (a c) -> p a c", a=2)
                A("sp", (lambda tb=tb, csv=csv: nc.sync.dma_start(out=csv, in_=rope_d[:, :, tb * 512:(tb + 1) * 512].rearrange("a p c -> p a c"))), writes=csb, dma=True)
                for (w_, w_b, dst, dstb) in ((w_qq, w_qqb, qT, qTb), (w_kk, w_kkb, kT, kTb)):
                    b1 = bank % 4; bank += 1; b2 = bank % 4; bank += 1
                    proj_fm(w_, w_b, 0, b1, tb); proj_fm(w_, w_b, 128, b2, tb)
                    r1, r1b = R1[(bank // 2) % 2]; r2, r2b = R2[(bank // 2) % 2]
                    A("dve", (lambda b1=b1, r1=r1, csv=csv: nc.vector.tensor_tensor(out=r1, in0=ps[:, b1, :], in1=csv[:, 0, :], op=ALU.mult)), reads=[pb[b1]] + csb, writes=r1b)
                    A("dve", (lambda b2=b2, r2=r2, csv=csv: nc.vector.tensor_tensor(out=r2, in0=ps[:, b2, :], in1=csv[:, 1, :], op=ALU.mult)), reads=[pb[b2]] + csb, writes=r2b)
                    A("pool", (lambda r1=r1, r2=r2, dst=dst, tb=tb: nc.gpsimd.tensor_tensor(out=dst[:, tb * 512:(tb + 1) * 512], in0=r1, in1=r2, op=ALU.add)), reads=r1b + r2b, writes=dstb)
            if RSTOP < 2: return
            qdec = cst[:, CST_QDEC + h * 128:CST_QDEC + (h + 1) * 128]
            A("pool", lambda: nc.gpsimd.tensor_tensor(out=qd.rearrange("p (c i) -> p c i", i=128), in0=qT.rearrange("p (c i) -> p c i", i=128), in1=qdec.unsqueeze(1).broadcast_to([128, 16, 128]), op=ALU.mult),
              reads=qTb + [cstb], writes=qdb)
            if RSTOP < 3: return
            for c4 in range(4):
                b = bank % 4; bank += 1
                for cc in range(4):
                    c = c4 * 4 + cc
                    A("pe", (lambda c=c, cc=cc, b=b: nc.tensor.transpose(psb[b][:, cc * 128:(cc + 1) * 128], kT[:, c * 128:(c + 1) * 128], id_bf)), reads=kTb + [cbfb], writes=[pb[b]])
                A("act", (lambda c4=c4, b=b: nc.scalar.activation(out=kd[:, c4 * 512:(c4 + 1) * 512], in_=psb[b][:, 0:512], func=AF.Identity, scale=cst[:, CST_KDEC + h:CST_KDEC + h + 1])), reads=[pb[b], cstb], writes=kdb)
            if RSTOP < 4: return
            for t2 in range(8):
                b = bank % 4; bank += 1
                for tt in range(2):
                    proj_tm(w_vg, w_vgb, 0, 256, b, tt * 256, t2 * 2 + tt)
                pv = ps[:, b, :].rearrange("p (a c) -> p a c", c=256)
                if VG >= 1:
                    A("dve", (lambda t2=t2, pv=pv: nc.vector.tensor_copy(out=v[:, t2 * 2:t2 * 2 + 2, 0:128], in_=pv[:, :, 0:128])), reads=[pb[b]], writes=[vb])
                if VG >= 2:
                  A("act", (lambda t2=t2, pv=pv: nc.scalar.activation(out=sgt.rearrange("p (c e) -> p c e", e=128)[:, t2 * 2:t2 * 2 + 2, :], in_=pv[:, :, 128:256], func=(AF.Silu if VG == 2 else AF.Copy))), reads=[pb[b], vb], writes=sgtb)
            if RSTOP < 5: return
            decT = cst[:, CST_DEC + h * 128:CST_DEC + (h + 1) * 128]
            for c in range(16):
                sb_ = 4 + c % 2; ob = 6 if c % 2 == 0 else 2; spb = 7 if c % 2 == 0 else 3
                sd, sdb = SD[c % 2]; sdv = sd[:, 0:128]
                A("pe", (lambda c=c, sb_=sb_: nc.tensor.matmul(ps[:, sb_, 0:128], lhsT=kT[:, c * 128:(c + 1) * 128], rhs=qT[:, c * 128:(c + 1) * 128], start=True, stop=True)), reads=kTb + qTb, writes=[pb[sb_]])
                A("dve", (lambda sb_=sb_, sdv=sdv: nc.vector.tensor_tensor(out=sdv, in0=ps[:, sb_, 0:128], in1=decT, op=ALU.mult)), reads=[pb[sb_], cstb], writes=sdb)
                A("pe", (lambda c=c, sdv=sdv, ob=ob: nc.tensor.matmul(ps[:, ob, 0:128], lhsT=sdv, rhs=v[:, c, 0:128], start=True, stop=(c == 0))), reads=sdb + [vb], writes=[pb[ob]])
                if c > 0:
                    A("pe", (lambda c=c, ob=ob: nc.tensor.matmul(ps[:, ob, 0:128], lhsT=qd[:, c * 128:(c + 1) * 128], rhs=stbf[:, 0:128], start=False, stop=True)), reads=qdb + stbfb, writes=[pb[ob]])
                if c < 15:
                    A("pe", (lambda c=c, spb=spb: nc.tensor.matmul(ps[:, spb, 0:128], lhsT=kd[:, c * 128:(c + 1) * 128], rhs=v[:, c, 0:128], start=True, stop=True)), reads=kdb + [vb], writes=[pb[spb]])
                    if c == 0:
                        A("dve", (lambda spb=spb: nc.vector.tensor_copy(out=stf[:, 0:128], in_=ps[:, spb, 0:128])), reads=[pb[spb]], writes=stfb)
                    else:
                        A("dve", (lambda spb=spb: nc.vector.scalar_tensor_tensor(out=stf[:, 0:128], in0=stf[:, 0:128], scalar=chunk_g[h], in1=ps[:, spb, 0:128], op0=ALU.mult, op1=ALU.add)), reads=[pb[spb]] + stfb, writes=stfb)
                    A("pool", lambda: nc.gpsimd.tensor_copy(out=stbf[:, 0:128], in_=stf[:, 0:128]), reads=stfb, writes=stbfb)
                A("dve", (lambda c=c, ob=ob: nc.vector.bn_stats(out=bnst[:, c, :], in_=ps[:, ob, 0:128])), reads=[pb[ob]], writes=[bnstb])
                A("act", (lambda c=c, ob=ob: nc.scalar.activation(out=oraw[:, c * 128:(c + 1) * 128], in_=ps[:, ob, 0:128], func=AF.Copy)), reads=[pb[ob]], writes=orawb)
            if RSTOP < 6: return
            for c in range(16):
                A("dve", (lambda c=c: nc.vector.bn_aggr(out=rstat[:, c, 0:2], in_=bnst[:, c, :])), reads=[bnstb], writes=[rstatb])
            A("dve", lambda: nc.vector.tensor_scalar(out=rstat[:, :, 2:3], in0=rstat[:, :, 1:2], scalar1=EPS, scalar2=None, op0=ALU.add), reads=[rstatb], writes=[rstatb])
            A("pool", lambda: nc.gpsimd.tensor_tensor(out=rstat[:, :, 3:4], in0=rstat[:, :, 2:3], in1=mhalf[:].unsqueeze(1).broadcast_to([128, 16, 1]), op=ALU.pow), reads=[rstatb, mhalfb], writes=[rstatb])
            for c in range(16):
                A("dve", (lambda c=c: nc.vector.tensor_scalar(out=oraw[:, c * 128:(c + 1) * 128], in0=oraw[:, c * 128:(c + 1) * 128], scalar1=rstat[:, c, 0:1], scalar2=rstat[:, c, 3:4], op0=ALU.subtract, op1=ALU.mult)),
                  reads=orawb + [rstatb], writes=orawb)
            A("pool", lambda: nc.gpsimd.tensor_tensor(out=ytok, in0=oraw, in1=sgt, op=ALU.mult), reads=orawb + sgtb, writes=ytokb)
            for c4 in range(4):
                b = c4 % 2
                for cc in range(4):
                    c = c4 * 4 + cc
                    A("pe", (lambda c=c, cc=cc, b=b: nc.tensor.transpose(psb[b][:, cc * 128:(cc + 1) * 128], ytok[:, c * 128:(c + 1) * 128], id_bf)), reads=ytokb + [cbfb], writes=[pb[b]])
                evac_copy(ya[:, h, c4 * 512:(c4 + 1) * 512], psb[b][:, 0:512], [pb[b]], [yab[h]], eng="act")

        def ffn(l, s, last):
            off = _offsets(l)
            CU = [at(32 + 2 * i, 2) for i in range(2)]; CG = [at(36 + 2 * i, 2) for i in range(2)]
            SG = [at(40 + 2 * i, 2) for i in range(2)]
            aT = ya_flat[:, 0:NP * 512].rearrange("p (k t) -> p k t", t=512)
            wdv = wbig[:, 0:22528].rearrange("p (k c) -> p k c", k=NP)
            load_ln(l, 1)
            it = 0
            cvl = convp[:, l * NP * 8:(l + 1) * NP * 8].rearrange("p (n u j) -> p n u j", u=2, j=4)
            for tb in range(4):
                if tb > 0:
                    A("dve", lambda: nc.vector.tensor_tensor(out=fx[:], in0=halo[:], in1=cvl[:, :, :, 0:1].broadcast_to([128, NP, 2, 2]), op=ALU.mult), reads=halob + [convpb], writes=[fxb])
                    A("dve", lambda: nc.vector.tensor_tensor(out=fxt[:], in0=halo[:, :, :, 1:2], in1=cvl[:, :, :, 1:2], op=ALU.mult), reads=halob + [convpb], writes=[fxtb])
                    A("dve", lambda: nc.vector.tensor_tensor(out=fx[:, :, :, 0:1], in0=fx[:, :, :, 0:1], in1=fxt[:], op=ALU.add), reads=[fxb, fxtb], writes=[fxb])
                for p in range(NP):
                    wt, wtb = load_w(l, off["up"](p), 2048, 8)
                    if tb == 0 and p == 2:
                        A("sp", lambda: nc.sync.dma_start(out=wbig[:, 0:22528], in_=wb[l][:, off["down"]:off["down"] + 22528]), reads=wb_bufs(l, off["down"], 22528), writes=[wbigb], dma=True)
                    bu = (it % 2) * 2; bg = bu + 1; it += 1
                    proj_fm(wt, wtb, 0, bu, tb); proj_fm(wt, wtb, 128, bg, tb)
                    res = []
                    for (bk, ug, CC) in ((bu, 0, CU), (bg, 1, CG)):
                        c_, cb_ = CC[it % 2]
                        cw = lambda j, ug=ug, p=p: convp[:, ((l * NP + p) * 2 + ug) * 4 + j:((l * NP + p) * 2 + ug) * 4 + j + 1]
                        A("act", (lambda bk=bk, c_=c_, cw=cw: nc.scalar.activation(out=c_, in_=ps[:, bk, :], func=AF.Identity, scale=cw(2), bias=cw(3))), reads=[pb[bk], convpb], writes=cb_)
                        A("dve", (lambda bk=bk, c_=c_, cw=cw: nc.vector.scalar_tensor_tensor(out=c_[:, 1:512], in0=ps[:, bk, 0:511], scalar=cw(1), in1=c_[:, 1:512], op0=ALU.mult, op1=ALU.add)), reads=[pb[bk], convpb] + cb_, writes=cb_)
                        A("dve", (lambda bk=bk, c_=c_, cw=cw: nc.vector.scalar_tensor_tensor(out=c_[:, 2:512], in0=ps[:, bk, 0:510], scalar=cw(0), in1=c_[:, 2:512], op0=ALU.mult, op1=ALU.add)), reads=[pb[bk], convpb] + cb_, writes=cb_)
                        if tb > 0:
                            A("pool", (lambda c_=c_, p=p, ug=ug: nc.gpsimd.tensor_tensor(out=c_[:, 0:2], in0=c_[:, 0:2], in1=fx[:, p, ug, :], op=ALU.add)), reads=[fxb] + cb_, writes=cb_)
                        if tb < 3:
                            A("act", (lambda bk=bk, p=p, ug=ug: nc.scalar.activation(out=halo[:, p, ug, :], in_=ps[:, bk, 510:512], func=AF.Copy)), reads=[pb[bk]], writes=[halob[p]])
                        res.append((c_, cb_))
                    (cu, cub), (cg, cgb) = res
                    sg, sgb = SG[it % 2]
                    A("act", (lambda cg=cg, sg=sg: nc.scalar.activation(out=sg, in_=cg, func=AF.Silu)), reads=cgb, writes=sgb)
                    A("pool", (lambda cu=cu, sg=sg, p=p: nc.gpsimd.tensor_tensor(out=aT[:, p, :], in0=cu, in1=sg, op=ALU.mult)), reads=cub + sgb, writes=[yab[p // 4]])
                tiles = [tb * 4 + i for i in range(4)]
                ln_phase(tiles, (lambda k, t, tb=tb: aT[:, k, (t - tb * 4) * 128:(t - tb * 4 + 1) * 128]), yab[0:6], NP, wdv,
                         (lambda t: (xres[s, t * 128:(t + 1) * 128, :], [xresb[s][t]])),
                         (lambda t: ((out_d if last else xres)[s, t * 128:(t + 1) * 128, :], [(outb if last else xresb)[s][t]])), not last)

        xresb = [[Buf("xres%d_%d" % (s, t)) for t in range(NT)] for s in range(2)]
        outb = [[Buf("out%d_%d" % (s, t)) for t in range(NT)] for s in range(2)]
        uidx = 0
        for s in range(nseq if dbg >= 1 else 0):
            for t in range(NT):
                xn, xnb = ln_tiles[("xn", t % 2)]
                A("sp", (lambda t=t, xn=xn, s=s: nc.sync.dma_start(out=xn, in_=x_d[s, t * 128:(t + 1) * 128, :])), writes=xnb, dma=True)
                for k in range(8):
                    A("pe", (lambda k=k, xn=xn: nc.tensor.transpose(ps[:, 6 + k // 4, (k % 4) * 128:(k % 4 + 1) * 128], xn[:, k * 128:(k + 1) * 128], id_f)), reads=xnb + [cstb], writes=[pb[6 + k // 4]])
                ptr = ps[:, 6:8, :].rearrange("p a (b c) -> p (a b) c", c=128)
                A("act", (lambda t=t, ptr=ptr: nc.scalar.activation(out=xT[:, :, t * 128:(t + 1) * 128], in_=ptr, func=AF.Copy)), reads=[pb[6], pb[7]], writes=[xTb[t]])
            first = True
            for li, l in enumerate(layers):
                off = _offsets(l)
                last = (li == len(layers) - 1)
                if l % 2 == 0:
                    units = [("ret", h) for h in range(4)] + [("sb", u) for u in range(4)]
                else:
                    units = [("diff", h) for h in range(8)]
                if dbg < 2:
                    break
                if dbg == 2:
                    units = units[:1]
                elif dbg == 3:
                    units = units[:4]
                for ui, (kind, idx) in enumerate(units):
                    if kind == "ret":
                        ret_unit(l, idx, uidx)
                    elif kind == "sb":
                        sb_unit(l, idx, uidx)
                    else:
                        diff_unit(l, idx, uidx)
                    uidx += 1
                    if ui == 1:
                        A("sp", (lambda l=l, off=off: nc.sync.dma_start(out=wbig[:, 0:8192], in_=wb[l][:, off["wout"]:off["wout"] + 8192])), reads=wb_bufs(l, off["wout"], 8192), writes=[wbigb], dma=True)
                        load_ln(l, 0)
                if dbg < 5:
                    break
                wov = wbig[:, 0:8192].rearrange("p (k c) -> p k c", k=8)
                srcT = x_d if first else xres
                ln_phase(list(range(NT)), (lambda k, t: ya[:, k, t * 128:(t + 1) * 128]), yab, 8, wov,
                         (lambda t, srcT=srcT, first=first: (srcT[s, t * 128:(t + 1) * 128, :], [] if first else [xresb[s][t]])),
                         (lambda t: (xres[s, t * 128:(t + 1) * 128, :], [xresb[s][t]])), True)
                first = False
                if dbg < 6:
                    break
                ffn(l, s, last)
        allout = [b for s in range(nseq) for b in outb[s]]
        A("sp", None, reads=[], writes=allout)
        info = S.emit()
    return nc, info


def _prep_inputs(inp):
    inp = {k: np.asarray(v) for k, v in inp.items()}
    cst, _ = _consts()
    shared = {"cst": cst, "rope": _rope(), "bt": _bias_tiles(inp["rel_bias"])}
    for l in range(DEPTH):
        shared["wl%d" % l] = _layer_weights(l, inp)
    cp = np.zeros((128, DEPTH, NP, 2, 4), np.float32)
    for l in range(DEPTH):
        for ug in range(2):
            for j in range(3):
                cp[:, l, :, ug, j] = inp["conv_w"][l, j, ug * DFF:(ug + 1) * DFF].reshape(NP, 128).T
            cp[:, l, :, ug, 3] = inp["conv_b"][l, ug * DFF:(ug + 1) * DFF].reshape(NP, 128).T
    shared["convp"] = cp.reshape(128, -1)
    ln = np.stack([np.stack([inp["ln1_g"][l], inp["ln1_b"][l], inp["ln2_g"][l], inp["ln2_b"][l]]) for l in range(DEPTH)]).reshape(16, D)
    shared["lnbc"] = np.ascontiguousarray(np.broadcast_to(ln[None], (128, 16, D)), dtype=np.float32)
    lam = np.stack([np.stack([inp["lam_q1"][i], inp["lam_k1"][i], inp["lam_q2"][i], inp["lam_k2"][i]]) for i in range(2)]).reshape(-1)
    shared["lamp"] = np.ascontiguousarray(np.broadcast_to(lam[None], (128, 512)), dtype=np.float32)
    shared["subg"] = np.ascontiguousarray(inp["subln_g"].T, dtype=np.float32)
    return inp, shared


_NC_CACHE = {}


def kernel(**inputs):
    inp, shared = _prep_inputs(inputs)
    if "nc" not in _NC_CACHE:
        _NC_CACHE["nc"] = build()[0]
    nc = _NC_CACHE["nc"]
    x = np.ascontiguousarray(inp["x"], dtype=np.float32)
    in_maps = []
    for c in range(8):
        m = dict(shared); m["x"] = np.ascontiguousarray(x[2 * c:2 * c + 2])
        in_maps.append(m)
    res = run_bass_kernel_spmd(nc, in_maps, core_ids=list(range(8)))
    return np.concatenate([r["out"] for r in res.results], axis=0).astype(np.float32)
```

```python
import math, contextlib
import numpy as np
import concourse.bass as bass
import concourse.mybir as mybir
from concourse.bass_utils import run_bass_kernel_spmd

F32 = mybir.dt.float32
BF16 = mybir.dt.bfloat16
AF = mybir.ActivationFunctionType
ALU = mybir.AluOpType

D = 1024; SEQ = 2048; NT = 16; DFF = 2816; NP = 22; DEPTH = 4
ALPHA = (2 * DEPTH) ** 0.25
EPS = 1e-5
NEG = -30000.0
FL_EVEN = 112640; FL_ODD = 100352
PCH = 2048
import os
RSTOP = int(os.environ.get('RSTOP', '99'))
VG = int(os.environ.get('VG', '2'))
DSTOP = int(os.environ.get('DSTOP', '99'))
SBSKEW = int(os.environ.get('SBSKEW', '1'))
DSKEW = int(os.environ.get('DSKEW', '1'))


class Buf:
    __slots__ = ("name", "lw", "rd", "rdd", "excl")

    def __init__(self, name="", excl=False):
        self.name = name; self.lw = None; self.rd = {}; self.rdd = []; self.excl = excl


class Op:
    __slots__ = ("eng", "fn", "deps", "dma", "signal", "sem", "val", "guard")

    def __init__(self, eng, fn, deps, dma):
        self.eng = eng; self.fn = fn; self.deps = deps; self.dma = dma
        self.signal = False; self.sem = None; self.val = 0; self.guard = None


class Sched:
    def __init__(self, nc, es, n_dma_sems=32, same_engine_sync=True):
        self.nc = nc; self.ops = []; self.same = same_engine_sync
        self.h = {"pe": nc.tensor, "act": nc.scalar, "dve": nc.vector, "pool": nc.gpsimd, "sp": nc.sync}
        self.esem = {e: es.enter_context(nc.semaphore("sem_" + e)) for e in ["pe", "act", "dve", "pool"]}
        self.dsems = [es.enter_context(nc.semaphore("dsem%d" % i)) for i in range(n_dma_sems)]

    def add(self, eng, fn, reads=(), writes=(), dma=False):
        i = len(self.ops)
        deps = set()
        for b in reads:
            if b.lw is not None:
                deps.add(b.lw)
            if b.excl:
                for e2, o2 in b.rd.items():
                    if e2 != eng:
                        deps.add(o2)
        for b in writes:
            if b.lw is not None:
                deps.add(b.lw)
            deps.update(b.rd.values()); deps.update(b.rdd)
        deps.discard(i)
        for b in reads:
            if dma:
                b.rdd.append(i)
            else:
                b.rd[eng] = i
        for b in writes:
            b.lw = i; b.rd = {}; b.rdd = []
        self.ops.append(Op(eng, fn, deps, dma))
        return i

    def _skip(self, dop, op):
        if dop.fn is None:
            assert dop.eng == op.eng, "cross-engine dep on barrier"
            return True
        if (not dop.dma) and (not op.dma) and dop.eng == op.eng:
            if dop.eng == "pe" or not self.same:
                return True
        return False

    def emit(self):
        ops = self.ops
        for op in ops:
            for d in op.deps:
                if not self._skip(ops[d], op):
                    ops[d].signal = True
        for op in ops:
            if op.dma and op.fn is not None:
                op.signal = True
        cnt = {e: 0 for e in self.esem}
        dval = [0] * len(self.dsems); rr = 0
        for op in ops:
            if not op.signal:
                continue
            if op.dma:
                op.sem = self.dsems[rr]; op.guard = (self.dsems[rr], dval[rr])
                dval[rr] += 16; op.val = dval[rr]; rr = (rr + 1) % len(self.dsems)
            else:
                cnt[op.eng] += 1; op.sem = self.esem[op.eng]; op.val = cnt[op.eng]
        waited = {e: {} for e in self.h}
        nw = 0
        for op in ops:
            e = self.h[op.eng]
            need = {}
            for d in op.deps:
                dop = ops[d]
                if self._skip(dop, op):
                    continue
                k = id(dop.sem)
                if k not in need or need[k][1] < dop.val:
                    need[k] = (dop.sem, dop.val)
            if op.dma and op.signal and op.guard[1] > 0:
                k = id(op.guard[0])
                if k not in need or need[k][1] < op.guard[1]:
                    need[k] = op.guard
            w = waited[op.eng]
            for k, (sem, val) in need.items():
                if w.get(k, 0) < val:
                    e.wait_ge(sem, val); w[k] = val; nw += 1
            if op.fn is None:
                continue
            inst = op.fn()
            if op.signal:
                inst.then_inc(op.sem, 16 if op.dma else 1)
        return dict(n_ops=len(ops), n_waits=nw, cnt=cnt, dval=max(dval))


def _tile_k(w):
    K = w.shape[0] // 128
    return np.ascontiguousarray(w.reshape(K, 128, -1).transpose(1, 0, 2)).reshape(128, -1)


def _layer_weights(l, inp):
    parts = []
    i = l // 2
    if l % 2 == 0:
        win = inp["w_in_even"][i]; wout = inp["w_out_even"][i]
        for h in range(4):
            q = win[:, h * 128:(h + 1) * 128]; qs = np.concatenate([q[:, 64:], q[:, :64]], 1)
            k = win[:, 512 + h * 128:512 + (h + 1) * 128]; ks = np.concatenate([k[:, 64:], k[:, :64]], 1)
            v = win[:, 1024 + h * 128:1024 + (h + 1) * 128]; g = win[:, 1536 + h * 128:1536 + (h + 1) * 128]
            for pair in ([q, qs], [k, ks], [v, g]):
                parts.append(_tile_k(np.concatenate(pair, 1)))
        for u in range(4):
            q = win[:, 2048 + u * 128:2048 + (u + 1) * 128]; k = win[:, 2560 + u * 128:2560 + (u + 1) * 128]
            v = win[:, 3072 + u * 128:3072 + (u + 1) * 128]
            parts.append(_tile_k(np.concatenate([q, k], 1))); parts.append(_tile_k(v))
    else:
        win = inp["w_in_odd"][i]; wout = inp["w_out_odd"][i]
        for h in range(8):
            q = win[:, h * 128:(h + 1) * 128]; k = win[:, 1024 + h * 128:1024 + (h + 1) * 128]
            v = win[:, 2048 + h * 128:2048 + (h + 1) * 128]
            parts.append(_tile_k(np.concatenate([q, k], 1))); parts.append(_tile_k(v))
    parts.append(_tile_k(wout))
    wup = inp["w_up"][l]
    for p in range(NP):
        parts.append(_tile_k(np.concatenate([wup[:, p * 128:(p + 1) * 128], wup[:, DFF + p * 128:DFF + (p + 1) * 128]], 1)))
    parts.append(_tile_k(inp["w_down"][l]))
    out = np.ascontiguousarray(np.concatenate(parts, 1), dtype=np.float32)
    assert out.shape[1] == (FL_EVEN if l % 2 == 0 else FL_ODD)
    return out


def _offsets(l):
    if l % 2 == 0:
        return dict(ret=lambda h, j: h * 6144 + j * 2048, qk=lambda u: 24576 + u * 3072, v=lambda u: 24576 + u * 3072 + 2048,
                    wout=36864, up=lambda p: 45056 + p * 2048, down=90112)
    return dict(qk=lambda u: u * 3072, v=lambda u: u * 3072 + 2048, wout=24576, up=lambda p: 32768 + p * 2048, down=77824)


CST_DEC = 0; CST_QDEC = 512; CST_KDEC = 1024; CST_SBM = 1028; CST_U = 1156; CST_ID = 1284; CST_W = 1412


def _consts():
    c = np.zeros((128, CST_W), np.float32)
    lg = np.log(1.0 - 2.0 ** (-5.0 - np.arange(4, dtype=np.float64)))
    idx = np.arange(128, dtype=np.float64)
    for h in range(4):
        rel = idx[None, :] - idx[:, None]
        dec = np.where(rel >= 0, np.exp(lg[h] * np.maximum(rel, 0.0)), 0.0) / math.sqrt(128.0)
        c[:, CST_DEC + h * 128:CST_DEC + (h + 1) * 128] = dec
        c[:, CST_QDEC + h * 128:CST_QDEC + (h + 1) * 128] = np.exp(lg[h] * (idx + 1.0))[None, :]
        c[:, CST_KDEC + h] = np.exp(lg[h] * (127.0 - idx)) / math.sqrt(128.0)
    s_i = idx[:, None]; i_i = idx[None, :]
    c[:, CST_SBM:CST_SBM + 128] = np.where(s_i < i_i, 0.0, NEG)
    c[:, CST_U:CST_U + 128] = (idx[:, None] > idx[None, :]).astype(np.float32)
    c[:, CST_ID:CST_ID + 128] = np.eye(128)
    chunk_g = [float(np.exp(lg[h] * 128.0)) for h in range(4)]
    return c, chunk_g


def _rope():
    inv = (10000.0 ** (-np.arange(0, 128, 2, dtype=np.float32) / np.float32(128))).astype(np.float32)
    ang = np.arange(SEQ, dtype=np.float32)[:, None] * inv[None, :]
    cos = np.cos(ang).T.astype(np.float32); sin = np.sin(ang).T.astype(np.float32)
    r = np.zeros((2, 128, SEQ), np.float32)
    r[0, :64] = cos; r[0, 64:] = cos
    r[1, :64] = -sin; r[1, 64:] = sin
    return r


def _t5_bucket(rel):
    n = np.maximum(rel, 0)
    nf = np.maximum(n, 1).astype(np.float32)
    large = 16 + (np.log(nf / np.float32(16)) / np.float32(math.log(128 / 16)) * np.float32(16)).astype(np.int32)
    large = np.minimum(large, 31)
    return np.where(n < 16, n, large)


def _bias_tiles(rel_bias):
    s = np.arange(128)[:, None]; i = np.arange(128)[None, :]
    b0 = _t5_bucket(i - s); b1 = _t5_bucket(128 + i - s)
    bt = np.zeros((128, 8, 2, 128), np.float32)
    for h in range(8):
        bt[:, h, 0, :] = np.where(i >= s, rel_bias[b0, h], NEG)
        bt[:, h, 1, :] = rel_bias[b1, h]
    c31 = np.broadcast_to(rel_bias[31][None, :], (128, 8))
    return np.ascontiguousarray(np.concatenate([bt.reshape(128, -1), c31], 1), dtype=np.float32)


def build(layers=(0, 1, 2, 3), nseq=2, same_sync=True, dbg=99):
    nc = bass.Bass("TRN2", target_bir_lowering=False, dynamic_dma_scratch_size=256)
    cst_np, chunk_g = _consts()
    x_d = nc.dram_tensor("x", [2, SEQ, D], F32, kind="ExternalInput").ap()
    out_d = nc.dram_tensor("out", [2, SEQ, D], F32, kind="ExternalOutput").ap()
    xres = nc.dram_tensor("xres", [2, SEQ, D], F32, kind="Internal").ap()
    wl = {}; wb = {}
    for l in layers:
        FL = FL_EVEN if l % 2 == 0 else FL_ODD
        wl[l] = nc.dram_tensor("wl%d" % l, [128, FL], F32, kind="ExternalInput").ap()
        wb[l] = nc.dram_tensor("wb%d" % l, [128, FL], BF16, kind="Internal").ap()
    cst_d = nc.dram_tensor("cst", [128, CST_W], F32, kind="ExternalInput").ap()
    rope_d = nc.dram_tensor("rope", [2, 128, SEQ], F32, kind="ExternalInput").ap()
    bt_d = nc.dram_tensor("bt", [128, 2056], F32, kind="ExternalInput").ap()
    convp_d = nc.dram_tensor("convp", [128, DEPTH * NP * 8], F32, kind="ExternalInput").ap()
    lnbc_d = nc.dram_tensor("lnbc", [128, 16, D], F32, kind="ExternalInput").ap()
    lamp_d = nc.dram_tensor("lamp", [128, 2 * 4 * 64], F32, kind="ExternalInput").ap()
    subg_d = nc.dram_tensor("subg", [128, 2], F32, kind="ExternalInput").ap()

    es = contextlib.ExitStack()
    with es:
        S = Sched(nc, es, same_engine_sync=same_sync)
        A = S.add

        def sb(n, shp, dt=F32):
            return es.enter_context(nc.sbuf_tensor("s_" + n, shp, dt))

        ps = es.enter_context(nc.psum_tensor("ps", [128, 8, 512], F32))
        pb = [Buf("pb%d" % i, excl=True) for i in range(8)]
        psb = [ps[:, i, :].bitcast(BF16) for i in range(8)]

        xT = sb("xT", [128, 8, SEQ], BF16); xTb = [Buf("xT%d" % t) for t in range(NT)]
        ya = sb("ya", [128, 8, SEQ], BF16); yab = [Buf("ya%d" % c) for c in range(8)]
        ya_flat = ya[:].rearrange("p k t -> p (k t)")
        wbig = sb("wbig", [128, 22528], BF16); wbigb = Buf("wbig")
        NSLOT = 4
        wsl = [sb("wsl%d" % i, [128, 2048], BF16) for i in range(NSLOT)]; wslb = [Buf("wsl%d" % i) for i in range(NSLOT)]
        slot_rr = [0]
        cst = sb("cst", [128, CST_W]); cstb = Buf("cst")
        cbf = sb("cbf", [128, 384], BF16); cbfb = Buf("cbf")
        U_bf = cbf[:, 0:128]; ones_bf = cbf[:, 128:256]; id_bf = cbf[:, 256:384]
        id_f = cst[:, CST_ID:CST_ID + 128]
        btt = sb("btt", [128, 2056]); bttb = Buf("btt")
        convp = sb("convp", [128, DEPTH * NP * 8]); convpb = Buf("convp")
        lnt = sb("lnt", [128, 2, D]); lntb = Buf("lnt")
        lamp = sb("lamp", [128, 512]); lampb = Buf("lamp")
        subg = sb("subg", [128, 2]); subgb = Buf("subg")
        lamt = sb("lamt", [128, 8]); lamtb = Buf("lamt")
        mhalf = sb("mhalf", [128, 1]); mhalfb = Buf("mhalf")
        vsb = [sb("vsb%d" % i, [128, 16, 130], BF16) for i in range(2)]; vsbb = [Buf("vsb%d" % i) for i in range(2)]
        halo = sb("halo", [128, NP, 2, 2]); halob = [Buf("halo%d" % p) for p in range(NP)]
        fx = sb("fx", [128, NP, 2, 2]); fxb = Buf("fx")
        fxt = sb("fxt", [128, NP, 2, 1]); fxtb = Buf("fxt")
        NST = 8
        stt = [sb("stt%d" % i, [128, 16]) for i in range(NST)]; sttb = [Buf("stt%d" % i) for i in range(NST)]
        st_rr = [0]
        rstat = sb("rstat", [128, 16, 8]); rstatb = Buf("rstat")
        bnst = sb("bnst", [128, 16, 6]); bnstb = Buf("bnst")
        bnd = sb("bnd", [128, 8]); bndb = Buf("bnd")
        NAR = 60
        arena = sb("arena", [128, NAR * 256]); arb = [Buf("ar%d" % i) for i in range(NAR)]

        def at(u0, nun, dt=F32):
            a = arena[:, u0 * 256:(u0 + nun) * 256]
            if dt == BF16:
                a = a.bitcast(BF16)
            return a, arb[u0:u0 + nun]

        def stat():
            i = st_rr[0]; st_rr[0] = (i + 1) % NST
            return stt[i], sttb[i]

        def wb_bufs(l, off, ln):
            return wbb[l][off // PCH:(off + ln - 1) // PCH + 1]

        def load_w(l, off, ln, K):
            i = slot_rr[0]; slot_rr[0] = (i + 1) % NSLOT
            dst = wsl[i][:, 0:ln]
            A("sp", lambda: nc.sync.dma_start(out=dst, in_=wb[l][:, off:off + ln]), reads=wb_bufs(l, off, ln), writes=[wslb[i]], dma=True)
            return wsl[i][:, 0:ln].rearrange("p (k c) -> p k c", k=K), wslb[i]

        A("sp", lambda: nc.sync.dma_start(out=cst[:], in_=cst_d[:, :]), writes=[cstb], dma=True)
        A("sp", lambda: nc.sync.dma_start(out=btt[:], in_=bt_d[:, :]), writes=[bttb], dma=True)
        A("sp", lambda: nc.sync.dma_start(out=convp[:], in_=convp_d[:, :]), writes=[convpb], dma=True)
        A("sp", lambda: nc.sync.dma_start(out=lamp[:], in_=lamp_d[:, :]), writes=[lampb], dma=True)
        A("sp", lambda: nc.sync.dma_start(out=subg[:], in_=subg_d[:, :]), writes=[subgb], dma=True)
        A("dve", lambda: nc.vector.tensor_copy(out=cbf[:, 0:128], in_=cst[:, CST_U:CST_U + 128]), reads=[cstb], writes=[cbfb])
        A("dve", lambda: nc.vector.memset(cbf[:, 128:256], 1.0), writes=[cbfb])
        A("dve", lambda: nc.vector.tensor_copy(out=cbf[:, 256:384], in_=cst[:, CST_ID:CST_ID + 128]), reads=[cstb], writes=[cbfb])
        A("dve", lambda: nc.vector.memset(mhalf[:], -0.5), writes=[mhalfb])
        for i in range(2):
            A("dve", (lambda i=i: nc.vector.memset(vsb[i][:, :, 128:130], 1.0)), writes=[vsbb[i]])
        for l in layers:
            if l % 2 == 1:
                i = l // 2
                lam_init = 0.8 - 0.6 * math.exp(-0.3 * l)
                t, tb_ = stat()
                pr = sb("lamprod%d" % i, [128, 128])
                prb = Buf()
                base = i * 256
                A("dve", (lambda base=base, pr=pr: nc.vector.tensor_tensor(out=pr[:, 0:64], in0=lamp[:, base:base + 64], in1=lamp[:, base + 64:base + 128], op=ALU.mult)), reads=[lampb], writes=[prb])
                A("dve", (lambda base=base, pr=pr: nc.vector.tensor_tensor(out=pr[:, 64:128], in0=lamp[:, base + 128:base + 192], in1=lamp[:, base + 192:base + 256], op=ALU.mult)), reads=[lampb, prb], writes=[prb])
                A("dve", (lambda pr=pr, t=t: nc.vector.reduce_sum(out=t[:, 0:2], in_=pr[:].rearrange("p (a b) -> p a b", a=2), axis=mybir.AxisListType.X)), reads=[prb], writes=[tb_])
                A("act", (lambda t=t: nc.scalar.activation(out=t[:, 2:4], in_=t[:, 0:2], func=AF.Exp)), reads=[tb_], writes=[tb_])
                A("dve", (lambda t=t, i=i, li=lam_init: nc.vector.scalar_tensor_tensor(out=lamt[:, i:i + 1], in0=t[:, 3:4], scalar=-li, in1=t[:, 2:3], op0=ALU.add, op1=ALU.subtract)), reads=[tb_], writes=[lamtb])

        wbb = {}
        st32 = [wbig[:, i * 4096:(i + 1) * 4096].bitcast(F32) for i in range(3)]
        st16 = [wbig[:, 12288 + i * 2048:12288 + (i + 1) * 2048] for i in range(3)]
        st32b = [Buf() for _ in range(3)]; st16b = [Buf() for _ in range(3)]
        chunks = []
        for l in layers:
            FL = FL_EVEN if l % 2 == 0 else FL_ODD
            wbb[l] = [Buf("wb%d_%d" % (l, c)) for c in range(FL // PCH)]
            chunks += [(l, c) for c in range(FL // PCH)]

        def prep_load(idx):
            l, c = chunks[idx]; j = idx % 3
            A("sp", (lambda: nc.sync.dma_start(out=st32[j], in_=wl[l][:, c * PCH:(c + 1) * PCH])), writes=[st32b[j]], dma=True)

        def prep_cast_store(idx):
            l, c = chunks[idx]; j = idx % 3
            if idx % 2 == 0:
                A("dve", (lambda: nc.vector.tensor_copy(out=st16[j], in_=st32[j])), reads=[st32b[j]], writes=[st16b[j]])
            else:
                A("pool", (lambda: nc.gpsimd.tensor_copy(out=st16[j], in_=st32[j])), reads=[st32b[j]], writes=[st16b[j]])
            A("sp", (lambda: nc.sync.dma_start(out=wb[l][:, c * PCH:(c + 1) * PCH], in_=st16[j])), reads=[st16b[j]], writes=[wbb[l][c]], dma=True)

        for idx in range(min(2, len(chunks))):
            prep_load(idx)
        for idx in range(len(chunks)):
            if idx + 2 < len(chunks):
                prep_load(idx + 2)
            prep_cast_store(idx)
        A("sp", None, reads=[], writes=st32b + st16b + [wbigb])

        evac_rr = [0]

        def evac_copy(out, in_, reads, writes, eng=None):
            if eng is None:
                eng = "act" if evac_rr[0] % 2 == 0 else "dve"; evac_rr[0] += 1
            if eng == "act":
                A("act", lambda: nc.scalar.activation(out=out, in_=in_, func=AF.Copy), reads=reads, writes=writes)
            else:
                A("dve", lambda: nc.vector.tensor_copy(out=out, in_=in_), reads=reads, writes=writes)

        def proj_fm(wt, wtb, col0, bank, tb):
            for k in range(8):
                A("pe", (lambda k=k: nc.tensor.matmul(ps[:, bank, :], lhsT=wt[:, k, col0:col0 + 128], rhs=xT[:, k, tb * 512:(tb + 1) * 512], start=(k == 0), stop=(k == 7))),
                  reads=[wtb] + xTb[tb * 4:tb * 4 + 4], writes=[pb[bank]])

        def proj_tm(wt, wtb, col0, ncol, bank, boff, t):
            for k in range(8):
                A("pe", (lambda k=k: nc.tensor.matmul(ps[:, bank, boff:boff + ncol], lhsT=xT[:, k, t * 128:(t + 1) * 128], rhs=wt[:, k, col0:col0 + ncol], start=(k == 0), stop=(k == 7))),
                  reads=[wtb, xTb[t]], writes=[pb[bank]])

        ln_tiles = {}
        for nm, u0 in (("xt", 0), ("z", 8), ("zn", 16), ("xn", 24)):
            for j in range(2):
                ln_tiles[(nm, j)] = at(u0 + 4 * j, 4)

        def ln_stage_a_pe(t, lhs_fn, lhs_bufs, K, wv, src, srcb):
            j = t % 2
            xt, xtb = ln_tiles[("xt", j)]
            for half in range(2):
                for k in range(K):
                    A("pe", (lambda half=half, k=k: nc.tensor.matmul(ps[:, 4 + half, :], lhsT=lhs_fn(k), rhs=wv[:, k, half * 512:(half + 1) * 512], start=(k == 0), stop=(k == K - 1))),
                      reads=lhs_bufs + [wbigb], writes=[pb[4 + half]])
            A("sp", lambda: nc.sync.dma_start(out=xt, in_=src), reads=srcb, writes=xtb, dma=True)

        def ln_stage_a_rest(t):
            j = t % 2
            xt, xtb = ln_tiles[("xt", j)]; z, zb = ln_tiles[("z", j)]
            pz = ps[:, 4:6, :].rearrange("p a b -> p (a b)")
            A("dve", lambda: nc.vector.scalar_tensor_tensor(out=z, in0=xt, scalar=ALPHA, in1=pz, op0=ALU.mult, op1=ALU.add), reads=xtb + [pb[4], pb[5]], writes=zb)
            st, stb = stat()
            for half in range(2):
                A("dve", (lambda half=half: nc.vector.bn_stats(out=bnst[:, half, :], in_=z[:, half * 512:(half + 1) * 512])), reads=zb, writes=[bnstb])
            A("dve", lambda: nc.vector.bn_aggr(out=st[:, 0:2], in_=bnst[:, 0:2, :].rearrange("p a b -> p (a b)")), reads=[bnstb], writes=[stb])
            A("dve", lambda: nc.vector.tensor_scalar(out=st[:, 2:3], in0=st[:, 1:2], scalar1=EPS, scalar2=None, op0=ALU.add), reads=[stb], writes=[stb])
            A("pool", lambda: nc.gpsimd.tensor_tensor(out=st[:, 3:4], in0=st[:, 2:3], in1=mhalf[:], op=ALU.pow), reads=[stb, mhalfb], writes=[stb])
            A("dve", lambda: nc.vector.scalar_tensor_tensor(out=st[:, 4:5], in0=st[:, 0:1], scalar=-1.0, in1=st[:, 3:4], op0=ALU.mult, op1=ALU.mult), reads=[stb], writes=[stb])
            return (t, j, st, stb)

        def ln_stage_b(state, dst, dstb, do_transpose):
            t, j, st, stb = state
            z, zb = ln_tiles[("z", j)]; zn, znb = ln_tiles[("zn", j)]; xn, xnb = ln_tiles[("xn", j)]
            A("act", lambda: nc.scalar.activation(out=zn, in_=z, func=AF.Identity, scale=st[:, 3:4], bias=st[:, 4:5]), reads=zb + [stb], writes=znb)
            A("dve", lambda: nc.vector.tensor_tensor(out=zn, in0=zn, in1=lnt[:, 0, :], op=ALU.mult), reads=znb + [lntb], writes=znb)
            A("pool", lambda: nc.gpsimd.tensor_tensor(out=xn, in0=zn, in1=lnt[:, 1, :], op=ALU.add), reads=znb + [lntb], writes=xnb)
            A("sp", lambda: nc.sync.dma_start(out=dst, in_=xn), reads=xnb, writes=dstb, dma=True)
            if do_transpose:
                for k in range(8):
                    A("pe", (lambda k=k: nc.tensor.transpose(ps[:, 6 + k // 4, (k % 4) * 128:(k % 4 + 1) * 128], xn[:, k * 128:(k + 1) * 128], id_f)),
                      reads=xnb + [cstb], writes=[pb[6 + k // 4]])
                ptr = ps[:, 6:8, :].rearrange("p a (b c) -> p (a b) c", c=128)
                A("act", lambda: nc.scalar.activation(out=xT[:, :, t * 128:(t + 1) * 128], in_=ptr, func=AF.Copy), reads=[pb[6], pb[7]], writes=[xTb[t]])

        def ln_phase(tiles, lhs_fn_t, lhs_bufs, K, wv, src_fn, dst_fn, do_transpose):
            pend = None
            for t in tiles:
                src, srcb = src_fn(t)
                ln_stage_a_pe(t, (lambda k, t=t: lhs_fn_t(k, t)), lhs_bufs, K, wv, src, srcb)
                if pend is not None:
                    d, db = dst_fn(pend[0]); ln_stage_b(pend, d, db, do_transpose)
                pend = ln_stage_a_rest(t)
            d, db = dst_fn(pend[0]); ln_stage_b(pend, d, db, do_transpose)

        def load_ln(l, which):
            A("sp", lambda: nc.sync.dma_start(out=lnt[:], in_=lnbc_d[:, l * 4 + which * 2:l * 4 + which * 2 + 2, :]), writes=[lntb], dma=True)

        def attn_proj(l, u, uidx):
            off = _offsets(l)
            base = (uidx % 2) * 12
            qz = [at(base, 4, BF16), at(base + 4, 4, BF16)]
            kT, kTb = at(base + 8, 4, BF16)
            wqk, wqkb = load_w(l, off["qk"](u), 2048, 8)
            wv_, wvb = load_w(l, off["v"](u), 1024, 8)
            A("pool", lambda: nc.gpsimd.memset(qz[0][0][64:128, :], 0.0), writes=qz[0][1])
            A("pool", lambda: nc.gpsimd.memset(qz[1][0][0:64, :], 0.0), writes=qz[1][1])
            bank = 0
            for tb in range(4):
                b = bank % 3; bank += 1
                proj_fm(wqk, wqkb, 0, b, tb)
                A("act", (lambda tb=tb, b=b: nc.scalar.activation(out=qz[0][0][0:64, tb * 512:(tb + 1) * 512], in_=ps[0:64, b, :], func=AF.Copy)), reads=[pb[b]], writes=qz[0][1])
                A("dve", (lambda tb=tb, b=b: nc.vector.tensor_copy(out=qz[1][0][64:128, tb * 512:(tb + 1) * 512], in_=ps[64:128, b, :])), reads=[pb[b]], writes=qz[1][1])
            for tb in range(4):
                b = bank % 3; bank += 1
                proj_fm(wqk, wqkb, 128, b, tb)
                evac_copy(kT[:, tb * 512:(tb + 1) * 512], ps[:, b, :], [pb[b]], kTb)
            vi = uidx % 2
            for t4 in range(4):
                b = bank % 3; bank += 1
                for tt in range(4):
                    proj_tm(wv_, wvb, 0, 128, b, tt * 128, t4 * 4 + tt)
                evac_copy(vsb[vi][:, t4 * 4:t4 * 4 + 4, 0:128], ps[:, b, :].rearrange("p (a c) -> p a c", c=128), [pb[b]], [vsbb[vi]])
            return qz, kT, kTb, vsb[vi], vsbb[vi]

        def diff_unit(l, h, uidx):
            i_odd = l // 2
            lam_init = 0.8 - 0.6 * math.exp(-0.3 * l)
            csc = (1.0 - lam_init) ** -2
            qz, kT, kTb, v, vb = attn_proj(l, h, uidx)
            if DSTOP < 2: return
            PT = [at(24 + i, 1, BF16) for i in range(3)]
            TMP = [at(27 + i, 1) for i in range(4)]
            o0, o0b = at(31, 2)
            OW = [at(33 + i, 1) for i in range(4)]
            ONB = [at(37 + i, 1, BF16) for i in range(2)]
            cnt = dict(tmp=0, ow=0, on=0)
            steps = [(g, m, kb) for g in range(4) for m in range(2) for kb in range(0, 4 * g + 4)]
            n = len(steps)

            def geom(i):
                g, m, kb = steps[i]
                r = kb - 4 * g; c0 = max(r, 0) * 128
                return g, m, kb, r, c0, 512 - c0, i % 3

            def st_S(i):
                g, m, kb, r, c0, N, b = geom(i)
                q_, qb_ = qz[m]
                A("pe", lambda: nc.tensor.matmul(ps[:, b, 0:N], lhsT=kT[:, kb * 128:(kb + 1) * 128], rhs=q_[:, g * 512 + c0:(g + 1) * 512], start=True, stop=True),
                  reads=kTb + qb_, writes=[pb[b]])

            def st_P(i):
                g, m, kb, r, c0, N, b = geom(i)
                pt, ptb = PT[i % 3]
                j0 = max(r, 0)
                far0 = None
                for j in range(j0, 4):
                    d = 4 * g + j - kb
                    lc = j * 128 - c0
                    if d <= 1:
                        tm, tmb = TMP[cnt["tmp"] % 4]; cnt["tmp"] += 1
                        tmv = tm[:, 0:128]
                        bto = h * 256 + d * 128
                        A("dve", (lambda lc=lc, tmv=tmv, bto=bto: nc.vector.scalar_tensor_tensor(out=tmv, in0=ps[:, b, lc:lc + 128], scalar=0.125, in1=btt[:, bto:bto + 128], op0=ALU.mult, op1=ALU.add)),
                          reads=[pb[b], bttb], writes=tmb)
                        A("act", (lambda lc=lc, tmv=tmv: nc.scalar.activation(out=pt[:, lc:lc + 128], in_=tmv, func=AF.Exp)), reads=tmb, writes=ptb)
                    elif far0 is None:
                        far0 = lc
                if far0 is not None:
                    A("act", (lambda far0=far0: nc.scalar.activation(out=pt[:, far0:N], in_=ps[:, b, far0:N], func=AF.Exp, scale=0.125, bias=btt[:, 2048 + h:2049 + h])),
                      reads=[pb[b], bttb], writes=ptb)
                for j in range(j0, 4):
                    lc = j * 128 - c0
                    A("pe", (lambda j=j, lc=lc: nc.tensor.matmul(ps[:, 3 + j, 0:129], lhsT=pt[:, lc:lc + 128], rhs=v[:, kb, 0:129], start=(kb == 0), stop=(kb == 4 * g + j))),
                      reads=ptb + [vb], writes=[pb[3 + j]])
                if kb == 4 * g + 3:
                    finalize(g, m)

            def finalize(g, m):
                for j in range(4):
                    st, stb = stat()
                    A("dve", (lambda j=j, st=st: nc.vector.reciprocal(out=st[:, 0:1], in_=ps[:, 3 + j, 128:129])), reads=[pb[3 + j]], writes=[stb])
                    if m == 0:
                        A("act", (lambda j=j, st=st: nc.scalar.activation(out=o0[:, j * 128:(j + 1) * 128], in_=ps[:, 3 + j, 0:128], func=AF.Identity, scale=st[:, 0:1])), reads=[pb[3 + j], stb], writes=o0b)
                    else:
                        ow, owb = OW[cnt["ow"] % 4]; cnt["ow"] += 1
                        o1 = ow[:, 0:128]; oo = ow[:, 128:256]
                        A("act", (lambda j=j, st=st, o1=o1: nc.scalar.activation(out=o1, in_=ps[:, 3 + j, 0:128], func=AF.Identity, scale=st[:, 0:1])), reads=[pb[3 + j], stb], writes=owb)
                        A("dve", (lambda j=j, o1=o1, oo=oo: nc.vector.scalar_tensor_tensor(out=oo, in0=o1, scalar=lamt[:, i_odd:i_odd + 1], in1=o0[:, j * 128:(j + 1) * 128], op0=ALU.mult, op1=ALU.add)),
                          reads=owb + o0b + [lamtb], writes=owb)
                        A("dve", (lambda oo=oo: nc.vector.bn_stats(out=bnd[:, 0:6], in_=oo)), reads=owb, writes=[bndb])
                        A("dve", (lambda st=st: nc.vector.bn_aggr(out=st[:, 4:6], in_=bnd[:, 0:6])), reads=[bndb], writes=[stb])
                        A("dve", (lambda st=st: nc.vector.scalar_tensor_tensor(out=st[:, 1:2], in0=st[:, 4:5], scalar=st[:, 4:5], in1=st[:, 5:6], op0=ALU.mult, op1=ALU.add)), reads=[stb], writes=[stb])
                        A("dve", (lambda st=st: nc.vector.tensor_scalar(out=st[:, 2:3], in0=st[:, 1:2], scalar1=csc, scalar2=EPS * csc, op0=ALU.mult, op1=ALU.add)), reads=[stb], writes=[stb])
                        A("pool", (lambda st=st: nc.gpsimd.tensor_tensor(out=st[:, 3:4], in0=st[:, 2:3], in1=mhalf[:], op=ALU.pow)), reads=[stb, mhalfb], writes=[stb])
                        onb, onbb = ONB[cnt["on"] % 2]; cnt["on"] += 1
                        onv = onb[:, 0:128]
                        A("act", (lambda st=st, oo=oo, onv=onv: nc.scalar.activation(out=onv, in_=oo, func=AF.Identity, scale=st[:, 3:4])), reads=owb + [stb], writes=onbb)
                        A("pe", (lambda j=j, onv=onv: nc.tensor.transpose(psb[7][:, j * 128:(j + 1) * 128], onv, id_bf)), reads=onbb + [cbfb], writes=[pb[7]])
                if m == 1:
                    A("act", lambda: nc.scalar.activation(out=ya[:, h, g * 512:(g + 1) * 512], in_=psb[7][:, 0:512], func=AF.Identity, scale=subg[:, i_odd:i_odd + 1]), reads=[pb[7], subgb], writes=[yab[h]])

            for i in range(min(DSKEW, n)):
                st_S(i)
            for i in range(n):
                if i + DSKEW < n:
                    st_S(i + DSKEW)
                st_P(i)

        def sb_unit(l, u, uidx):
            qz, kT, kTb, v, vb = attn_proj(l, u, uidx)
            AT = [at(24 + i, 1, BF16) for i in range(3)]
            LB = [at(27 + i, 1, BF16) for i in range(3)]
            E = [at(30 + 2 * i, 2) for i in range(2)]
            TT = [at(34 + 2 * i, 2) for i in range(2)]
            ZD = [at(38 + i, 1) for i in range(2)]
            sbm = cst[:, CST_SBM:CST_SBM + 128]
            steps = []
            gi = 0
            for g in range(4):
                for a in range(2):
                    for kb in range(4 * g + 3, -1, -1):
                        steps.append((g, a, kb, 6 + gi % 2))
                    gi += 1
            n = len(steps)

            def geom(i):
                g, a, kb, ob = steps[i]
                r = kb - 4 * g; c0 = max(r, 0) * 128; N = 512 - c0
                nd = 128 if r >= 0 else 0
                return g, a, kb, ob, r, c0, N, nd

            def st_Z(i):
                g, a, kb, ob, r, c0, N, nd = geom(i)
                b = i % 3
                q_, qb_ = qz[a]
                A("pe", lambda: nc.tensor.matmul(ps[:, b, 0:N], lhsT=kT[:, kb * 128:(kb + 1) * 128], rhs=q_[:, g * 512 + c0:(g + 1) * 512], start=True, stop=True),
                  reads=kTb + qb_, writes=[pb[b]])

            def st_EL(i):
                g, a, kb, ob, r, c0, N, nd = geom(i)
                zb_ = i % 3; e, eb = E[i % 2]; lb, lbb = LB[i % 3]; cb = 3 + i % 2
                if r >= 0:
                    zd, zdb = ZD[i % 2]; zdv = zd[:, 0:128]
                    A("dve", lambda: nc.vector.scalar_tensor_tensor(out=zdv, in0=ps[:, zb_, 0:128], scalar=0.125, in1=sbm, op0=ALU.mult, op1=ALU.add), reads=[pb[zb_], cstb], writes=zdb)
                    A("act", lambda: nc.scalar.activation(out=e[:, 0:128], in_=zdv, func=AF.Exp), reads=zdb, writes=eb)
                if N > nd:
                    A("act", lambda: nc.scalar.activation(out=e[:, nd:N], in_=ps[:, zb_, nd:N], func=AF.Exp, scale=0.125), reads=[pb[zb_]], writes=eb)
                A("act", lambda: nc.scalar.activation(out=lb[:, 0:N], in_=e[:, 0:N], func=AF.Ln, bias=1.0), reads=eb, writes=lbb)
                A("pe", lambda: nc.tensor.matmul(ps[:, cb, 0:N], lhsT=U_bf, rhs=lb[:, 0:N], start=True, stop=True), reads=lbb + [cbfb], writes=[pb[cb]])

            def st_D(i):
                g, a, kb, ob, r, c0, N, nd = geom(i)
                zb_ = i % 3; lb, lbb = LB[i % 3]; cb = 3 + i % 2; tt, ttb = TT[i % 2]; att, attb = AT[i % 3]
                if r >= 0:
                    zd, zdb = ZD[i % 2]; zdv = zd[:, 0:128]
                    A("dve", lambda: nc.vector.tensor_tensor(out=tt[:, 0:128], in0=zdv, in1=lb[:, 0:128], op=ALU.subtract), reads=zdb + lbb, writes=ttb)
                if N > nd:
                    A("dve", lambda: nc.vector.scalar_tensor_tensor(out=tt[:, nd:N], in0=ps[:, zb_, nd:N], scalar=0.125, in1=lb[:, nd:N], op0=ALU.mult, op1=ALU.subtract),
                      reads=[pb[zb_]] + lbb, writes=ttb)
                A("dve", lambda: nc.vector.scalar_tensor_tensor(out=tt[:, 0:N], in0=ps[:, cb, 0:N], scalar=-1.0, in1=tt[:, 0:N], op0=ALU.mult, op1=ALU.add), reads=[pb[cb]] + ttb, writes=ttb)
                if N > nd and kb != 4 * g + 3:
                    A("dve", lambda: nc.vector.scalar_tensor_tensor(out=tt[:, nd:N], in0=ps[:, 5, c0 + nd:512], scalar=-1.0, in1=tt[:, nd:N], op0=ALU.mult, op1=ALU.add), reads=[pb[5]] + ttb, writes=ttb)
                if kb >= 1:
                    A("pe", lambda: nc.tensor.matmul(ps[:, 5, c0:512], lhsT=ones_bf, rhs=lb[:, 0:N], start=(kb == 4 * g + 3), stop=True, skip_group_check=True), reads=lbb + [cbfb], writes=[pb[5]])
                A("act", lambda: nc.scalar.activation(out=att[:, 0:N], in_=tt[:, 0:N], func=AF.Exp), reads=ttb, writes=attb)
                A("pe", lambda: nc.tensor.matmul(ps[:, ob, c0:512], lhsT=v[:, kb, 0:128], rhs=att[:, 0:N], start=(kb == 4 * g + 3), stop=(kb == 0), skip_group_check=True), reads=attb + [vb], writes=[pb[ob]])
                if kb == 0:
                    lo = a * 64
                    evac_copy(ya[lo:lo + 64, 4 + u, g * 512:(g + 1) * 512], ps[lo:lo + 64, ob, :], [pb[ob]], [yab[4 + u]])

            if SBSKEW:
                for i in range(-2, n):
                    if 0 <= i + 2 < n:
                        st_Z(i + 2)
                    if 0 <= i + 1 < n:
                        st_EL(i + 1)
                    if i >= 0:
                        st_D(i)
            else:
                for i in range(n):
                    st_Z(i); st_EL(i); st_D(i)

        def ret_unit(l, h, uidx):
            off = _offsets(l)
            qT, qTb = at((uidx % 2) * 8, 4, BF16); kT, kTb = at((uidx % 2) * 8 + 4, 4, BF16)
            qd, qdb = at(16, 4, BF16); kd, kdb = at(20, 4, BF16)
            oraw, orawb = at(24, 8); sgt, sgtb = at(32, 4, BF16); ytok, ytokb = at(36, 4, BF16)
            SD = [at(40 + i, 1, BF16) for i in range(2)]
            stf, stfb = at(42, 1); stbf, stbfb = at(43, 1, BF16)
            R1 = [at(44 + 2 * i, 2) for i in range(2)]; R2 = [at(48 + 2 * i, 2) for i in range(2)]
            CS = [at(52 + 4 * i, 4) for i in range(2)]
            w_qq, w_qqb = load_w(l, off["ret"](h, 0), 2048, 8)
            w_kk, w_kkb = load_w(l, off["ret"](h, 1), 2048, 8)
            w_vg, w_vgb = load_w(l, off["ret"](h, 2), 2048, 8)
            vi = uidx % 2; v = vsb[vi]; vb = vsbb[vi]
            bank = 0
            for tb in range(4):
                cs, csb = CS[tb % 2]
                csv = cs.rearrange("p (a c) -> p a c", a=2)
                A("sp", (lambda tb=tb, csv=csv: nc.sync.dma_start(out=csv, in_=rope_d[:, :, tb * 512:(tb + 1) * 512].rearrange("a p c -> p a c"))), writes=csb, dma=True)
                for (w_, w_b, dst, dstb) in ((w_qq, w_qqb, qT, qTb), (w_kk, w_kkb, kT, kTb)):
                    b1 = bank % 4; bank += 1; b2 = bank % 4; bank += 1
                    proj_fm(w_, w_b, 0, b1, tb); proj_fm(w_, w_b, 128, b2, tb)
                    r1, r1b = R1[(bank // 2) % 2]; r2, r2b = R2[(bank // 2) % 2]
                    A("dve", (lambda b1=b1, r1=r1, csv=csv: nc.vector.tensor_tensor(out=r1, in0=ps[:, b1, :], in1=csv[:, 0, :], op=ALU.mult)), reads=[pb[b1]] + csb, writes=r1b)
                    A("dve", (lambda b2=b2, r2=r2, csv=csv: nc.vector.tensor_tensor(out=r2, in0=ps[:, b2, :], in1=csv[:, 1, :], op=ALU.mult)), reads=[pb[b2]] + csb, writes=r2b)
                    A("pool", (lambda r1=r1, r2=r2, dst=dst, tb=tb: nc.gpsimd.tensor_tensor(out=dst[:, tb * 512:(tb + 1) * 512], in0=r1, in1=r2, op=ALU.add)), reads=r1b + r2b, writes=dstb)
            if RSTOP < 2: return
            qdec = cst[:, CST_QDEC + h * 128:CST_QDEC + (h + 1) * 128]
            A("pool", lambda: nc.gpsimd.tensor_tensor(out=qd.rearrange("p (c i) -> p c i", i=128), in0=qT.rearrange("p (c i) -> p c i", i=128), in1=qdec.unsqueeze(1).broadcast_to([128, 16, 128]), op=ALU.mult),
              reads=qTb + [cstb], writes=qdb)
            if RSTOP < 3: return
            for c4 in range(4):
                b = bank % 4; bank += 1
                for cc in range(4):
                    c = c4 * 4 + cc
                    A("pe", (lambda c=c, cc=cc, b=b: nc.tensor.transpose(psb[b][:, cc * 128:(cc + 1) * 128], kT[:, c * 128:(c + 1) * 128], id_bf)), reads=kTb + [cbfb], writes=[pb[b]])
                A("act", (lambda c4=c4, b=b: nc.scalar.activation(out=kd[:, c4 * 512:(c4 + 1) * 512], in_=psb[b][:, 0:512], func=AF.Identity, scale=cst[:, CST_KDEC + h:CST_KDEC + h + 1])), reads=[pb[b], cstb], writes=kdb)
            if RSTOP < 4: return
            for t2 in range(8):
                b = bank % 4; bank += 1
                for tt in range(2):
                    proj_tm(w_vg, w_vgb, 0, 256, b, tt * 256, t2 * 2 + tt)
                pv = ps[:, b, :].rearrange("p (a c) -> p a c", c=256)
                if VG >= 1:
                    A("dve", (lambda t2=t2, pv=pv: nc.vector.tensor_copy(out=v[:, t2 * 2:t2 * 2 + 2, 0:128], in_=pv[:, :, 0:128])), reads=[pb[b]], writes=[vb])
                if VG >= 2:
                  A("act", (lambda t2=t2, pv=pv: nc.scalar.activation(out=sgt.rearrange("p (c e) -> p c e", e=128)[:, t2 * 2:t2 * 2 + 2, :], in_=pv[:, :, 128:256], func=(AF.Silu if VG == 2 else AF.Copy))), reads=[pb[b], vb], writes=sgtb)
            if RSTOP < 5: return
            decT = cst[:, CST_DEC + h * 128:CST_DEC + (h + 1) * 128]
            for c in range(16):
                sb_ = 4 + c % 2; ob = 6 if c % 2 == 0 else 2; spb = 7 if c % 2 == 0 else 3
                sd, sdb = SD[c % 2]; sdv = sd[:, 0:128]
                A("pe", (lambda c=c, sb_=sb_: nc.tensor.matmul(ps[:, sb_, 0:128], lhsT=kT[:, c * 128:(c + 1) * 128], rhs=qT[:, c * 128:(c + 1) * 128], start=True, stop=True)), reads=kTb + qTb, writes=[pb[sb_]])
                A("dve", (lambda sb_=sb_, sdv=sdv: nc.vector.tensor_tensor(out=sdv, in0=ps[:, sb_, 0:128], in1=decT, op=ALU.mult)), reads=[pb[sb_], cstb], writes=sdb)
                A("pe", (lambda c=c, sdv=sdv, ob=ob: nc.tensor.matmul(ps[:, ob, 0:128], lhsT=sdv, rhs=v[:, c, 0:128], start=True, stop=(c == 0))), reads=sdb + [vb], writes=[pb[ob]])
                if c > 0:
                    A("pe", (lambda c=c, ob=ob: nc.tensor.matmul(ps[:, ob, 0:128], lhsT=qd[:, c * 128:(c + 1) * 128], rhs=stbf[:, 0:128], start=False, stop=True)), reads=qdb + stbfb, writes=[pb[ob]])
                if c < 15:
                    A("pe", (lambda c=c, spb=spb: nc.tensor.matmul(ps[:, spb, 0:128], lhsT=kd[:, c * 128:(c + 1) * 128], rhs=v[:, c, 0:128], start=True, stop=True)), reads=kdb + [vb], writes=[pb[spb]])
                    if c == 0:
                        A("dve", (lambda spb=spb: nc.vector.tensor_copy(out=stf[:, 0:128], in_=ps[:, spb, 0:128])), reads=[pb[spb]], writes=stfb)
                    else:
                        A("dve", (lambda spb=spb: nc.vector.scalar_tensor_tensor(out=stf[:, 0:128], in0=stf[:, 0:128], scalar=chunk_g[h], in1=ps[:, spb, 0:128], op0=ALU.mult, op1=ALU.add)), reads=[pb[spb]] + stfb, writes=stfb)
                    A("pool", lambda: nc.gpsimd.tensor_copy(out=stbf[:, 0:128], in_=stf[:, 0:128]), reads=stfb, writes=stbfb)
                A("dve", (lambda c=c, ob=ob: nc.vector.bn_stats(out=bnst[:, c, :], in_=ps[:, ob, 0:128])), reads=[pb[ob]], writes=[bnstb])
                A("act", (lambda c=c, ob=ob: nc.scalar.activation(out=oraw[:, c * 128:(c + 1) * 128], in_=ps[:, ob, 0:128], func=AF.Copy)), reads=[pb[ob]], writes=orawb)
            if RSTOP < 6: return
            for c in range(16):
                A("dve", (lambda c=c: nc.vector.bn_aggr(out=rstat[:, c, 0:2], in_=bnst[:, c, :])), reads=[bnstb], writes=[rstatb])
            A("dve", lambda: nc.vector.tensor_scalar(out=rstat[:, :, 2:3], in0=rstat[:, :, 1:2], scalar1=EPS, scalar2=None, op0=ALU.add), reads=[rstatb], writes=[rstatb])
            A("pool", lambda: nc.gpsimd.tensor_tensor(out=rstat[:, :, 3:4], in0=rstat[:, :, 2:3], in1=mhalf[:].unsqueeze(1).broadcast_to([128, 16, 1]), op=ALU.pow), reads=[rstatb, mhalfb], writes=[rstatb])
            for c in range(16):
                A("dve", (lambda c=c: nc.vector.tensor_scalar(out=oraw[:, c * 128:(c + 1) * 128], in0=oraw[:, c * 128:(c + 1) * 128], scalar1=rstat[:, c, 0:1], scalar2=rstat[:, c, 3:4], op0=ALU.subtract, op1=ALU.mult)),
                  reads=orawb + [rstatb], writes=orawb)
            A("pool", lambda: nc.gpsimd.tensor_tensor(out=ytok, in0=oraw, in1=sgt, op=ALU.mult), reads=orawb + sgtb, writes=ytokb)
            for c4 in range(4):
                b = c4 % 2
                for cc in range(4):
                    c = c4 * 4 + cc
                    A("pe", (lambda c=c, cc=cc, b=b: nc.tensor.transpose(psb[b][:, cc * 128:(cc + 1) * 128], ytok[:, c * 128:(c + 1) * 128], id_bf)), reads=ytokb + [cbfb], writes=[pb[b]])
                evac_copy(ya[:, h, c4 * 512:(c4 + 1) * 512], psb[b][:, 0:512], [pb[b]], [yab[h]], eng="act")

        def ffn(l, s, last):
            off = _offsets(l)
            CU = [at(32 + 2 * i, 2) for i in range(2)]; CG = [at(36 + 2 * i, 2) for i in range(2)]
            SG = [at(40 + 2 * i, 2) for i in range(2)]
            aT = ya_flat[:, 0:NP * 512].rearrange("p (k t) -> p k t", t=512)
            wdv = wbig[:, 0:22528].rearrange("p (k c) -> p k c", k=NP)
            load_ln(l, 1)
            it = 0
            cvl = convp[:, l * NP * 8:(l + 1) * NP * 8].rearrange("p (n u j) -> p n u j", u=2, j=4)
            for tb in range(4):
                if tb > 0:
                    A("dve", lambda: nc.vector.tensor_tensor(out=fx[:], in0=halo[:], in1=cvl[:, :, :, 0:1].broadcast_to([128, NP, 2, 2]), op=ALU.mult), reads=halob + [convpb], writes=[fxb])
                    A("dve", lambda: nc.vector.tensor_tensor(out=fxt[:], in0=halo[:, :, :, 1:2], in1=cvl[:, :, :, 1:2], op=ALU.mult), reads=halob + [convpb], writes=[fxtb])
                    A("dve", lambda: nc.vector.tensor_tensor(out=fx[:, :, :, 0:1], in0=fx[:, :, :, 0:1], in1=fxt[:], op=ALU.add), reads=[fxb, fxtb], writes=[fxb])
                for p in range(NP):
                    wt, wtb = load_w(l, off["up"](p), 2048, 8)
                    if tb == 0 and p == 2:
                        A("sp", lambda: nc.sync.dma_start(out=wbig[:, 0:22528], in_=wb[l][:, off["down"]:off["down"] + 22528]), reads=wb_bufs(l, off["down"], 22528), writes=[wbigb], dma=True)
                    bu = (it % 2) * 2; bg = bu + 1; it += 1
                    proj_fm(wt, wtb, 0, bu, tb); proj_fm(wt, wtb, 128, bg, tb)
                    res = []
                    for (bk, ug, CC) in ((bu, 0, CU), (bg, 1, CG)):
                        c_, cb_ = CC[it % 2]
                        cw = lambda j, ug=ug, p=p: convp[:, ((l * NP + p) * 2 + ug) * 4 + j:((l * NP + p) * 2 + ug) * 4 + j + 1]
                        A("act", (lambda bk=bk, c_=c_, cw=cw: nc.scalar.activation(out=c_, in_=ps[:, bk, :], func=AF.Identity, scale=cw(2), bias=cw(3))), reads=[pb[bk], convpb], writes=cb_)
                        A("dve", (lambda bk=bk, c_=c_, cw=cw: nc.vector.scalar_tensor_tensor(out=c_[:, 1:512], in0=ps[:, bk, 0:511], scalar=cw(1), in1=c_[:, 1:512], op0=ALU.mult, op1=ALU.add)), reads=[pb[bk], convpb] + cb_, writes=cb_)
                        A("dve", (lambda bk=bk, c_=c_, cw=cw: nc.vector.scalar_tensor_tensor(out=c_[:, 2:512], in0=ps[:, bk, 0:510], scalar=cw(0), in1=c_[:, 2:512], op0=ALU.mult, op1=ALU.add)), reads=[pb[bk], convpb] + cb_, writes=cb_)
                        if tb > 0:
                            A("pool", (lambda c_=c_, p=p, ug=ug: nc.gpsimd.tensor_tensor(out=c_[:, 0:2], in0=c_[:, 0:2], in1=fx[:, p, ug, :], op=ALU.add)), reads=[fxb] + cb_, writes=cb_)
                        if tb < 3:
                            A("act", (lambda bk=bk, p=p, ug=ug: nc.scalar.activation(out=halo[:, p, ug, :], in_=ps[:, bk, 510:512], func=AF.Copy)), reads=[pb[bk]], writes=[halob[p]])
                        res.append((c_, cb_))
                    (cu, cub), (cg, cgb) = res
                    sg, sgb = SG[it % 2]
                    A("act", (lambda cg=cg, sg=sg: nc.scalar.activation(out=sg, in_=cg, func=AF.Silu)), reads=cgb, writes=sgb)
                    A("pool", (lambda cu=cu, sg=sg, p=p: nc.gpsimd.tensor_tensor(out=aT[:, p, :], in0=cu, in1=sg, op=ALU.mult)), reads=cub + sgb, writes=[yab[p // 4]])
                tiles = [tb * 4 + i for i in range(4)]
                ln_phase(tiles, (lambda k, t, tb=tb: aT[:, k, (t - tb * 4) * 128:(t - tb * 4 + 1) * 128]), yab[0:6], NP, wdv,
                         (lambda t: (xres[s, t * 128:(t + 1) * 128, :], [xresb[s][t]])),
                         (lambda t: ((out_d if last else xres)[s, t * 128:(t + 1) * 128, :], [(outb if last else xresb)[s][t]])), not last)

        xresb = [[Buf("xres%d_%d" % (s, t)) for t in range(NT)] for s in range(2)]
        outb = [[Buf("out%d_%d" % (s, t)) for t in range(NT)] for s in range(2)]
        uidx = 0
        for s in range(nseq if dbg >= 1 else 0):
            for t in range(NT):
                xn, xnb = ln_tiles[("xn", t % 2)]
                A("sp", (lambda t=t, xn=xn, s=s: nc.sync.dma_start(out=xn, in_=x_d[s, t * 128:(t + 1) * 128, :])), writes=xnb, dma=True)
                for k in range(8):
                    A("pe", (lambda k=k, xn=xn: nc.tensor.transpose(ps[:, 6 + k // 4, (k % 4) * 128:(k % 4 + 1) * 128], xn[:, k * 128:(k + 1) * 128], id_f)), reads=xnb + [cstb], writes=[pb[6 + k // 4]])
                ptr = ps[:, 6:8, :].rearrange("p a (b c) -> p (a b) c", c=128)
                A("act", (lambda t=t, ptr=ptr: nc.scalar.activation(out=xT[:, :, t * 128:(t + 1) * 128], in_=ptr, func=AF.Copy)), reads=[pb[6], pb[7]], writes=[xTb[t]])
            first = True
            for li, l in enumerate(layers):
                off = _offsets(l)
                last = (li == len(layers) - 1)
                if l % 2 == 0:
                    units = [("ret", h) for h in range(4)] + [("sb", u) for u in range(4)]
                else:
                    units = [("diff", h) for h in range(8)]
                if dbg < 2:
                    break
                if dbg == 2:
                    units = units[:1]
                elif dbg == 3:
                    units = units[:4]
                for ui, (kind, idx) in enumerate(units):
                    if kind == "ret":
                        ret_unit(l, idx, uidx)
                    elif kind == "sb":
                        sb_unit(l, idx, uidx)
                    else:
                        diff_unit(l, idx, uidx)
                    uidx += 1
                    if ui == 1:
                        A("sp", (lambda l=l, off=off: nc.sync.dma_start(out=wbig[:, 0:8192], in_=wb[l][:, off["wout"]:off["wout"] + 8192])), reads=wb_bufs(l, off["wout"], 8192), writes=[wbigb], dma=True)
                        load_ln(l, 0)
                if dbg < 5:
                    break
                wov = wbig[:, 0:8192].rearrange("p (k c) -> p k c", k=8)
                srcT = x_d if first else xres
                ln_phase(list(range(NT)), (lambda k, t: ya[:, k, t * 128:(t + 1) * 128]), yab, 8, wov,
                         (lambda t, srcT=srcT, first=first: (srcT[s, t * 128:(t + 1) * 128, :], [] if first else [xresb[s][t]])),
                         (lambda t: (xres[s, t * 128:(t + 1) * 128, :], [xresb[s][t]])), True)
                first = False
                if dbg < 6:
                    break
                ffn(l, s, last)
        allout = [b for s in range(nseq) for b in outb[s]]
        A("sp", None, reads=[], writes=allout)
        info = S.emit()
    return nc, info


def _prep_inputs(inp):
    inp = {k: np.asarray(v) for k, v in inp.items()}
    cst, _ = _consts()
    shared = {"cst": cst, "rope": _rope(), "bt": _bias_tiles(inp["rel_bias"])}
    for l in range(DEPTH):
        shared["wl%d" % l] = _layer_weights(l, inp)
    cp = np.zeros((128, DEPTH, NP, 2, 4), np.float32)
    for l in range(DEPTH):
        for ug in range(2):
            for j in range(3):
                cp[:, l, :, ug, j] = inp["conv_w"][l, j, ug * DFF:(ug + 1) * DFF].reshape(NP, 128).T
            cp[:, l, :, ug, 3] = inp["conv_b"][l, ug * DFF:(ug + 1) * DFF].reshape(NP, 128).T
    shared["convp"] = cp.reshape(128, -1)
    ln = np.stack([np.stack([inp["ln1_g"][l], inp["ln1_b"][l], inp["ln2_g"][l], inp["ln2_b"][l]]) for l in range(DEPTH)]).reshape(16, D)
    shared["lnbc"] = np.ascontiguousarray(np.broadcast_to(ln[None], (128, 16, D)), dtype=np.float32)
    lam = np.stack([np.stack([inp["lam_q1"][i], inp["lam_k1"][i], inp["lam_q2"][i], inp["lam_k2"][i]]) for i in range(2)]).reshape(-1)
    shared["lamp"] = np.ascontiguousarray(np.broadcast_to(lam[None], (128, 512)), dtype=np.float32)
    shared["subg"] = np.ascontiguousarray(inp["subln_g"].T, dtype=np.float32)
    return inp, shared


_NC_CACHE = {}


def kernel(**inputs):
    inp, shared = _prep_inputs(inputs)
    if "nc" not in _NC_CACHE:
        _NC_CACHE["nc"] = build()[0]
    nc = _NC_CACHE["nc"]
    x = np.ascontiguousarray(inp["x"], dtype=np.float32)
    in_maps = []
    for c in range(8):
        m = dict(shared); m["x"] = np.ascontiguousarray(x[2 * c:2 * c + 2])
        in_maps.append(m)
    res = run_bass_kernel_spmd(nc, in_maps, core_ids=list(range(8)))
    return np.concatenate([r["out"] for r in res.results], axis=0).astype(np.float32)
```
